# Optimizing a Trainium2 kernel written in Bass

```python
import math
import jax, jax.numpy as jnp
from jax import lax
import numpy as np

D_MODEL = 1024
BATCH = 8
SEQ = 4096
DEPTH = 1

GRID_W = 64
CTX_LEN = 256
N_HEADS = 8
QK_NOPE_DIM = 64
QK_ROPE_DIM = 32
V_HEAD_DIM = 64
Q_LORA_RANK = 256
KV_LORA_RANK = 128
MLA_WIDTH = N_HEADS * V_HEAD_DIM
CONV_WIDTH = D_MODEL - MLA_WIDTH
CONV_K = 3
D_FF = 4 * D_MODEL
ROPE_THETA = 10000.0
ROPE_AXIS_DIM = QK_ROPE_DIM // 2
Q_BLOCK = 128
EPS = 1e-6
MLA_IN = Q_LORA_RANK + KV_LORA_RANK + QK_ROPE_DIM
IN_COLS = MLA_IN + 3 * CONV_WIDTH
QK_DIM = QK_NOPE_DIM + QK_ROPE_DIM
ATTN_SCALE = 1.0 / math.sqrt(QK_DIM)

kernel_name = 'hybrid_mla_shortconv_dit_layer'


def rmsnorm(x):
    xf = x.astype(jnp.float32)
    y = xf * lax.rsqrt(jnp.mean(xf * xf, axis=-1, keepdims=True) + EPS)
    return y.astype(x.dtype)


def modulate(x, shift, scale):
    return rmsnorm(x) * (1 + scale) + shift


def adaln(cvec, w_mod, b_mod):
    m = jax.nn.silu(cvec) @ w_mod + b_mod
    return jnp.split(m, 6, axis=-1)


def rope_tables(rows):
    row = jnp.broadcast_to(jnp.arange(rows)[:, None], (rows, GRID_W)).reshape(-1)
    col = jnp.broadcast_to(jnp.arange(GRID_W)[None, :], (rows, GRID_W)).reshape(-1)
    freqs = ROPE_THETA ** (-jnp.arange(0, ROPE_AXIS_DIM, 2, dtype=jnp.float32) / ROPE_AXIS_DIM)
    ang = jnp.stack([row.astype(jnp.float32)[:, None] * freqs,
                     col.astype(jnp.float32)[:, None] * freqs], axis=1)
    ang = ang[:, None]
    return jnp.cos(ang), jnp.sin(ang)


def apply_rope(x, cos, sin):
    xs = x.reshape(x.shape[:-1] + (2, 2, ROPE_AXIS_DIM // 2))
    x1, x2 = xs[..., 0, :], xs[..., 1, :]
    cos = cos.astype(x.dtype)
    sin = sin.astype(x.dtype)
    out = jnp.stack([x1 * cos - x2 * sin, x2 * cos + x1 * sin], axis=-2)
    return out.reshape(x.shape)


def mla_q(z, q_g, w_uq, cos, sin):
    cq = rmsnorm(z[..., :Q_LORA_RANK]) * q_g
    q = (cq @ w_uq).reshape(z.shape[:-1] + (N_HEADS, QK_DIM))
    q_nope, q_rope = q[..., :QK_NOPE_DIM], q[..., QK_NOPE_DIM:]
    if cos is not None:
        q_rope = apply_rope(q_rope, cos, sin)
    return jnp.concatenate([q_nope, q_rope], axis=-1)


def mla_kv(z, kv_g, w_ukv, cos, sin):
    ckv = rmsnorm(z[..., Q_LORA_RANK:Q_LORA_RANK + KV_LORA_RANK]) * kv_g
    k_rope = z[..., Q_LORA_RANK + KV_LORA_RANK:MLA_IN][..., None, :]
    kv = (ckv @ w_ukv).reshape(z.shape[:-1] + (N_HEADS, QK_NOPE_DIM + V_HEAD_DIM))
    k_nope, v = kv[..., :QK_NOPE_DIM], kv[..., QK_NOPE_DIM:]
    if cos is not None:
        k_rope = apply_rope(k_rope, cos, sin)
    k_rope = jnp.broadcast_to(k_rope, k_nope.shape[:-1] + (QK_ROPE_DIM,))
    return jnp.concatenate([k_nope, k_rope], axis=-1), v


def attention_dense(q, k, v):
    s = jnp.einsum('bqhd,bkhd->bhqk', q, k).astype(jnp.float32) * ATTN_SCALE
    p = jax.nn.softmax(s, axis=-1).astype(v.dtype)
    o = jnp.einsum('bhqk,bkhd->bqhd', p, v)
    return o.reshape(o.shape[:2] + (N_HEADS * V_HEAD_DIM,))


def attention_blocked(q, k, v):
    b, s = q.shape[0], q.shape[1]
    nblk = s // Q_BLOCK
    qb = q.reshape(b, nblk, Q_BLOCK, N_HEADS, QK_DIM).swapaxes(0, 1)
    o = lax.map(lambda qq: attention_dense(qq, k, v), qb)
    return o.swapaxes(0, 1).reshape(b, s, N_HEADS * V_HEAD_DIM)


def short_conv(z, conv_w):
    gb, gc, xin = jnp.split(z[..., MLA_IN:], 3, axis=-1)
    u = gc * xin
    n = u.shape[1]
    up = jnp.pad(u, ((0, 0), (1, 1), (0, 0)))
    y = conv_w[0] * up[:, :n] + conv_w[1] * up[:, 1:n + 1] + conv_w[2] * up[:, 2:n + 2]
    return gb * y


def sq_relu_mlp(h, w1, w2):
    return jnp.square(jax.nn.relu(h @ w1)) @ w2


def setup_inputs(seed: int = 0) -> dict:
    key = jax.random.key(seed)
    ks = jax.random.split(key, 16)
    f32 = jnp.float32
    n = lambda k, shape, s: jax.random.normal(k, shape, f32) * s
    return {
        'x': n(ks[0], (BATCH, SEQ, D_MODEL), 1.0),
        'c': n(ks[1], (BATCH, D_MODEL), 1.0),
        'ctx': n(ks[2], (BATCH, CTX_LEN, D_MODEL), 1.0),
        'c_ctx': n(ks[3], (D_MODEL,), 1.0),
        'w_mod': n(ks[4], (DEPTH, D_MODEL, 6 * D_MODEL), D_MODEL ** -0.5),
        'b_mod': n(ks[5], (DEPTH, 6 * D_MODEL), 0.02),
        'w_in': n(ks[6], (DEPTH, D_MODEL, IN_COLS), D_MODEL ** -0.5),
        'q_norm_g': 1.0 + n(ks[7], (DEPTH, Q_LORA_RANK), 0.1),
        'w_uq': n(ks[8], (DEPTH, Q_LORA_RANK, N_HEADS * QK_DIM), Q_LORA_RANK ** -0.5),
        'kv_norm_g': 1.0 + n(ks[9], (DEPTH, KV_LORA_RANK), 0.1),
        'w_ukv': n(ks[10], (DEPTH, KV_LORA_RANK, N_HEADS * (QK_NOPE_DIM + V_HEAD_DIM)), KV_LORA_RANK ** -0.5),
        'conv_w': n(ks[11], (DEPTH, CONV_K, CONV_WIDTH), CONV_K ** -0.5),
        'w_out': n(ks[12], (DEPTH, D_MODEL, D_MODEL), D_MODEL ** -0.5),
        'w_mlp1': n(ks[13], (DEPTH, D_MODEL, D_FF), D_MODEL ** -0.5),
        'w_mlp2': n(ks[14], (DEPTH, D_FF, D_MODEL), D_FF ** -0.5),
        'final_norm_g': 1.0 + n(ks[15], (D_MODEL,), 0.1),
    }


def reference(x, c, ctx, c_ctx, w_mod, b_mod, w_in, q_norm_g, w_uq, kv_norm_g, w_ukv,
              conv_w, w_out, w_mlp1, w_mlp2, final_norm_g):
    rows = x.shape[1] // GRID_W
    cos, sin = rope_tables(rows)
    ctx_s = ctx
    for i in range(DEPTH):
        sh1, sc1, g1, sh2, sc2, g2 = [m[:, None, :] for m in adaln(c, w_mod[i], b_mod[i])]
        sh1c, sc1c, g1c, sh2c, sc2c, g2c = adaln(c_ctx, w_mod[i], b_mod[i])

        z = modulate(x, sh1, sc1) @ w_in[i]
        zc = modulate(ctx_s, sh1c, sc1c) @ w_in[i]

        q = mla_q(z, q_norm_g[i], w_uq[i], cos, sin)
        k, v = mla_kv(z, kv_norm_g[i], w_ukv[i], cos, sin)
        kc, vc = mla_kv(zc, kv_norm_g[i], w_ukv[i], None, None)
        k_all = jnp.concatenate([k, kc], axis=1)
        v_all = jnp.concatenate([v, vc], axis=1)
        attn = attention_blocked(q, k_all, v_all)
        conv = short_conv(z, conv_w[i])
        x = x + g1 * (jnp.concatenate([attn, conv], axis=-1) @ w_out[i])

        x = x + g2 * sq_relu_mlp(modulate(x, sh2, sc2), w_mlp1[i], w_mlp2[i])

        if i + 1 < DEPTH:
            qc = mla_q(zc, q_norm_g[i], w_uq[i], None, None)
            attn_c = attention_dense(qc, kc, vc)
            conv_c = short_conv(zc, conv_w[i])
            ctx_s = ctx_s + g1c * (jnp.concatenate([attn_c, conv_c], axis=-1) @ w_out[i])
            ctx_s = ctx_s + g2c * sq_relu_mlp(modulate(ctx_s, sh2c, sc2c), w_mlp1[i], w_mlp2[i])

    return rmsnorm(x) * final_norm_g
```

```python
import math
import numpy as np
from contextlib import ExitStack
import concourse.bass as bass
import concourse.mybir as mybir
from concourse.bass_utils import run_bass_kernel_spmd

F32 = mybir.dt.float32
BF16 = mybir.dt.bfloat16
ALU = mybir.AluOpType
AF = mybir.ActivationFunctionType

D = 1024
S = 4096
L = 256
NH = 8
NKEY = S + L
NKT = NKEY // 128
INC = 1952
EPS = 1e-6
ATTN_SCALE = 1.0 / math.sqrt(96.0)

ENGS = ["pe", "act", "dve", "pool", "sp"]
N_DMA_SEMS = {"sp": 12, "pool": 10}


class Prog:
    def __init__(self):
        self.ops = []
        self.lw = {}
        self.rd = {}
        self.eng_ops = {e: [] for e in ENGS}
        self.ndma = {e: 0 for e in ENGS}
        self.pending = {}
        self.last_op = {}
        self.last_dma = {}

    def op(self, eng, fn, r=(), w=(), dma=False):
        i = len(self.ops)
        deps = set()
        for k in r:
            if k in self.lw:
                deps.add(self.lw[k])
        for k in w:
            if k in self.lw:
                deps.add(self.lw[k])
            deps.update(self.rd.get(k, ()))
        if eng in self.pending:
            deps |= self.pending.pop(eng)
        for k in r:
            self.rd.setdefault(k, []).append(i)
        for k in w:
            self.lw[k] = i
            self.rd[k] = []
        o = dict(eng=eng, fn=fn, deps=deps, dma=dma, sig=False)
        if dma:
            k = self.ndma[eng]
            o["dma_k"] = k
            self.ndma[eng] += 1
            self.last_dma[(eng, k % N_DMA_SEMS[eng])] = i
        else:
            self.last_op[eng] = i
        self.ops.append(o)
        self.eng_ops[eng].append(i)
        return i

    def dma(self, eng, out, in_, r=(), w=()):
        return self.op(eng, lambda e: e.dma_start(out=out, in_=in_), r=r, w=w, dma=True)

    def barrier(self):
        deps = set(self.last_op.values()) | set(self.last_dma.values())
        for e in ENGS:
            self.pending[e] = set(deps) | self.pending.get(e, set())

    def emit(self, nc, block, semstack):
        ops = self.ops
        engsem = {e: semstack.enter_context(nc.semaphore("s_" + e)) for e in ENGS}
        dmasem = {
            e: [semstack.enter_context(nc.semaphore("d_%s%d" % (e, i))) for i in range(n)]
            for e, n in N_DMA_SEMS.items()
        }
        for o in ops:
            for d in o["deps"]:
                dd = ops[d]
                if (not dd["dma"]) and dd["eng"] == o["eng"] and o["eng"] == "pe":
                    continue
                dd["sig"] = True
        cnt = {e: 0 for e in ENGS}
        for o in ops:
            if o["dma"]:
                n = N_DMA_SEMS[o["eng"]]
                o["sem"] = dmasem[o["eng"]][o["dma_k"] % n]
                o["val"] = 16 * (o["dma_k"] // n + 1)
            elif o["sig"]:
                cnt[o["eng"]] += 1
                o["sem"] = engsem[o["eng"]]
                o["val"] = cnt[o["eng"]]

        def run(engname, e, final=False):
            known = {}
            for i in self.eng_ops[engname]:
                o = ops[i]
                waits = {}
                for d in o["deps"]:
                    dd = ops[d]
                    if not dd["sig"] and not dd["dma"]:
                        continue
                    s, v = dd["sem"], dd["val"]
                    key = id(s)
                    if key not in waits or waits[key][1] < v:
                        waits[key] = (s, v)
                if o["dma"] and o["val"] > 16:
                    s = o["sem"]
                    key = id(s)
                    v = o["val"] - 16
                    if key not in waits or waits[key][1] < v:
                        waits[key] = (s, v)
                for key, (s, v) in waits.items():
                    if known.get(key, 0) < v:
                        e.wait_ge(s, v)
                        known[key] = v
                ins = o["fn"](e)
                if o["dma"]:
                    ins.then_inc(o["sem"], 16)
                elif o["sig"]:
                    ins.then_inc(o["sem"], 1)
            if final:
                last = {}
                for o in ops:
                    if o["dma"]:
                        key = id(o["sem"])
                        if key not in last or last[key][1] < o["val"]:
                            last[key] = (o["sem"], o["val"])
                for key, (s, v) in last.items():
                    if known.get(key, 0) < v:
                        e.wait_ge(s, v)

        @block.tensor
        def _(e):
            run("pe", e)

        @block.scalar
        def _(e):
            run("act", e)

        @block.vector
        def _(e):
            run("dve", e)

        @block.gpsimd
        def _(e):
            run("pool", e)

        @block.sync
        def _(e):
            run("sp", e, final=True)


class Arena:
    def __init__(self, t, base, cap):
        self.t = t
        self.base = base
        self.cap = cap
        self.off = 0

    def reset(self, off=0):
        self.off = off

    def alloc(self, shape, dt):
        esz = 4 if dt == F32 else 2
        ne = int(np.prod(shape))
        n = (ne * esz + 31) // 32 * 32
        o = self.base + self.off
        assert self.off + n <= self.cap, ("arena overflow", self.off, n, self.cap)
        self.off += n
        v = self.t[:, o // 4:(o + n) // 4]
        if dt != F32:
            v = v.bitcast(dt)
        v = v[:, 0:ne]
        if len(shape) == 2:
            v = v.rearrange("p (a b) -> p a b", b=shape[1])
        elif len(shape) == 3:
            v = v.rearrange("p (a b c) -> p a b c", b=shape[1], c=shape[2])
        return v


class H:
    def __init__(self, P):
        self.P = P
        self.rr = 0

    def dma(self, eng, out, in_, r=(), w=()):
        return self.P.op(eng, lambda e: e.dma_start(out=out, in_=in_), r=r, w=w, dma=True)

    def copy(self, eng, out, in_, r=(), w=()):
        if eng == "act":
            return self.P.op("act", lambda e: e.activation(out=out, in_=in_, func=AF.Copy), r=r, w=w)
        return self.P.op(eng, lambda e: e.tensor_copy(out, in_), r=r, w=w)

    def cast(self, out, in_, r=(), w=(), engs=("dve", "pool", "act")):
        e = engs[self.rr % len(engs)]
        self.rr += 1
        return self.copy(e, out, in_, r, w)

    def memset(self, eng, ap, val, r=(), w=()):
        return self.P.op(eng, lambda e: e.memset(ap, val), r=r, w=w)

    def tt(self, eng, out, in0, in1, op, r=(), w=()):
        return self.P.op(eng, lambda e: e.tensor_tensor(out=out, in0=in0, in1=in1, op=op), r=r, w=w)

    def ts(self, eng, out, in0, s1, s2, op0, op1=None, r=(), w=()):
        if op1 is None:
            return self.P.op(eng, lambda e: e.tensor_scalar(out=out, in0=in0, scalar1=s1, scalar2=None, op0=op0), r=r, w=w)
        return self.P.op(eng, lambda e: e.tensor_scalar(out=out, in0=in0, scalar1=s1, scalar2=s2, op0=op0, op1=op1), r=r, w=w)

    def stt(self, out, in0, scalar, in1, op0, op1, r=(), w=()):
        return self.P.op("dve", lambda e: e.scalar_tensor_tensor(out=out, in0=in0, scalar=scalar, in1=in1, op0=op0, op1=op1), r=r, w=w)

    def act(self, out, in_, func, r=(), w=(), scale=None, accum=None, bias=None):
        kw = {}
        if bias is not None:
            kw["bias"] = bias
        if scale is not None:
            kw["scale"] = scale
        if accum is not None:
            kw["accum_out"] = accum
        return self.P.op("act", lambda e: e.activation(out=out, in_=in_, func=func, **kw), r=r, w=w)

    def recip(self, out, in_, r=(), w=()):
        return self.P.op("dve", lambda e: e.reciprocal(out=out, in_=in_), r=r, w=w)

    def rsum(self, out, in_, r=(), w=()):
        return self.P.op("dve", lambda e: e.reduce_sum(out=out, in_=in_, axis=mybir.AxisListType.X), r=r, w=w)

    def mms(self, lst, r=(), w=()):
        lst = list(lst)

        def f(e):
            for (o, l, rh, s0, s1) in lst:
                ins = e.matmul(o, lhsT=l, rhs=rh, start=s0, stop=s1)
            return ins
        return self.P.op("pe", f, r=r, w=w)

    def trs(self, lst, ident, r=(), w=()):
        lst = list(lst)

        def f(e):
            for (o, i) in lst:
                ins = e.transpose(out=o, in_=i, identity=ident)
            return ins
        return self.P.op("pe", f, r=r, w=w)


P_BYTES = 23 * 1024
R1_BYTES = 80 * 1024
TOTAL_BYTES = 212800


def build(dbg=False):
    nc = bass.Bass("TRN2", target_bir_lowering=False)

    def din(name, shape, dt=F32):
        return nc.dram_tensor(name, shape, dt, kind="ExternalInput").ap()

    x = din("x", [S, D])
    ctx = din("ctx", [L, D])
    c_fm = din("c_fm", [128, 8])
    cc_fm = din("cc_fm", [128, 8])
    w_mod = din("w_mod", [D, 6 * D])
    b_mod = din("b_mod", [6 * D])
    w_in = din("w_in", [D, INC])
    w_kr_sw = din("w_kr_sw", [D, 32])
    qg_fm = din("qg_fm", [128, 2])
    w_uq = din("w_uq", [256, 768])
    w_uq_sw = din("w_uq_sw", [256, 256])
    kvg_fm = din("kvg_fm", [128, 1])
    w_ukv = din("w_ukv", [128, 1024])
    convw_fm = din("convw_fm", [128, 12])
    w_out = din("w_out", [D, D])
    w1 = din("w_mlp1", [D, 4096])
    w2 = din("w_mlp2", [4096, D])
    gf = din("gf", [D])
    ident = din("ident", [128, 128])
    cos_t = din("cos_t", [32, S])
    sin_t = din("sin_t", [32, S])
    y = nc.dram_tensor("y", [S, D], F32, kind="ExternalOutput").ap()
    acT = nc.dram_tensor("acT", [D, S], BF16, kind="Internal").ap()
    w1s = nc.dram_tensor("w1s", [32, 128, 1024], BF16, kind="Internal").ap()

    P = Prog()
    h_ = H(P)
    with ExitStack() as st:
        arena_t = st.enter_context(nc.sbuf_tensor("arena", [128, TOTAL_BYTES // 4], F32))
        AP_ = Arena(arena_t, 0, P_BYTES)
        R1 = Arena(arena_t, P_BYTES, R1_BYTES)
        R2 = Arena(arena_t, P_BYTES + R1_BYTES, TOTAL_BYTES - P_BYTES - R1_BYTES)
        SA = st.enter_context(nc.psum_tensor("SA", [128, 1024], F32))
        SB = st.enter_context(nc.psum_tensor("SB", [128, 1024], F32))
        O0 = st.enter_context(nc.psum_tensor("O0", [128, 512], F32))
        O1 = st.enter_context(nc.psum_tensor("O1", [128, 512], F32))
        M0 = st.enter_context(nc.psum_tensor("M0", [128, 512], F32))
        M1 = st.enter_context(nc.psum_tensor("M1", [128, 512], F32))
        block = st.enter_context(nc.Block())
        Sps = [SA[:, :], SB[:, :]]
        Ops = [O0[:, :], O1[:, :]]
        Mps = [M0[:, :], M1[:, :]]
        Z = [SA[:, 0:512], SA[:, 512:1024], SB[:, 0:512], SB[:, 512:1024]]
        ZK = ["Z0", "Z1", "Z2", "Z3"]

        idb = AP_.alloc([128], BF16)
        identf = AP_.alloc([128], F32)
        ones_bf = AP_.alloc([128], BF16)
        onesf = AP_.alloc([64], F32)
        mh = AP_.alloc([1], F32)
        epsb = AP_.alloc([1], F32)
        css = AP_.alloc([8], F32)
        vecs = AP_.alloc([8, 8], F32)
        qg = AP_.alloc([2], F32)
        kvg = AP_.alloc([1], F32)
        convw = AP_.alloc([12], F32)
        ssA = AP_.alloc([40], F32)
        ssB = AP_.alloc([40], F32)
        rsA = AP_.alloc([40], F32)
        gf_bc = AP_.alloc([1024], F32)
        g1_bc = AP_.alloc([1024], F32)
        g2_bc = AP_.alloc([1024], F32)
        w_uqb = AP_.alloc([2, 768], BF16)
        w_uqsb = AP_.alloc([2, 768], BF16)
        w_kb = AP_.alloc([512], BF16)
        w_vb = AP_.alloc([512], BF16)

        h_.dma("sp", identf, ident, w=["identf"])
        h_.copy("dve", idb, identf, r=["identf"], w=["idb"])
        h_.memset("pool", ones_bf, 1.0, w=["ones_bf"])
        h_.memset("pool", onesf, 1.0, w=["onesf"])
        h_.memset("pool", mh, -0.5, w=["mh"])
        h_.memset("pool", epsb, EPS, w=["epsb"])
        h_.dma("sp", qg, qg_fm, w=["qg"])
        h_.dma("sp", kvg, kvg_fm, w=["kvg"])
        h_.dma("sp", convw, convw_fm, w=["convw"])
        h_.dma("sp", gf_bc, gf.partition_broadcast(128), w=["gf_bc"])

        R1.reset(); R2.reset()
        cs = R2.alloc([8], F32)
        ccs = R2.alloc([8], F32)
        ccss = R2.alloc([8], F32)
        csb = R2.alloc([8, 128], BF16)
        ccb = R2.alloc([8, 128], BF16)
        bb = [R2.alloc([512], F32) for _ in range(3)]
        modbc = [R2.alloc([512], F32) for _ in range(3)]
        junkf = R2.alloc([128], F32)
        wmb = [R1.alloc([8, 512], BF16) for _ in range(3)]
        sstg = R2.alloc([2048], F32)
        sstg2 = R2.alloc([1024], F32)

        h_.dma("sp", cs, c_fm, w=["cs"])
        h_.dma("sp", ccs, cc_fm, w=["ccs"])
        h_.act(css, cs, AF.Silu, r=["cs"], w=["css"])
        h_.act(ccss, ccs, AF.Silu, r=["ccs"], w=["ccss"])
        for j in range(8):
            h_.copy("dve", csb[:, j, :], css[:, j:j + 1].to_broadcast([128, 128]), r=["css"], w=["csb"])
            h_.copy("dve", ccb[:, j, :], ccss[:, j:j + 1].to_broadcast([128, 128]), r=["ccss"], w=["ccb"])

        w_mod_v = w_mod.rearrange("(k p) n -> p k n", p=128)
        VIDX = {0: 0, 1: 1, 3: 4, 4: 5}

        def diag_extract(sl, half, vi):
            for jj in range(4):
                j = half * 4 + jj
                h_.tt("dve", junkf, modbc[sl][:, jj * 128:(jj + 1) * 128], identf, ALU.mult, r=["modbc%d" % sl, "identf"], w=["junkf"])
                h_.rsum(vecs[:, vi, j:j + 1], junkf, r=["junkf"], w=["vecs"])

        for nb in range(4):
            sl = nb % 3
            m, half = nb // 2, nb % 2
            for k in range(0, 8, 2):
                h_.dma("pool", wmb[sl][:, k:k + 2, :], w_mod_v[:, k:k + 2, nb * 512:(nb + 1) * 512], w=["wmb%d_%d" % (sl, kk) for kk in range(k, k + 2)])
            h_.dma("sp", bb[sl], b_mod[nb * 512:(nb + 1) * 512].partition_broadcast(128), w=["bb%d" % sl])
            zi = nb % 2
            h_.mms([(Z[zi], csb[:, k, :], wmb[sl][:, k, :], k == 0, k == 7) for k in range(8)],
                   r=["csb"] + ["wmb%d_%d" % (sl, k) for k in range(8)], w=[ZK[zi]])
            if m in (2, 5):
                gb_ = g1_bc if m == 2 else g2_bc
                h_.tt("dve", gb_[:, half * 512:(half + 1) * 512], Z[zi], bb[sl], ALU.add, r=[ZK[zi], "bb%d" % sl], w=["gbc%d_%d" % (m, half)])
            else:
                h_.tt("dve", modbc[sl], Z[zi], bb[sl], ALU.add, r=[ZK[zi], "bb%d" % sl], w=["modbc%d" % sl])
                diag_extract(sl, half, VIDX[m])
            if nb < 4:
                zi2 = 2 + nb % 2
                h_.mms([(Z[zi2], ccb[:, k, :], wmb[sl][:, k, :], k == 0, k == 7) for k in range(8)],
                       r=["ccb"] + ["wmb%d_%d" % (sl, k) for k in range(8)], w=[ZK[zi2]])
                h_.tt("dve", modbc[sl], Z[zi2], bb[sl], ALU.add, r=[ZK[zi2], "bb%d" % sl], w=["modbc%d" % sl])
                diag_extract(sl, half, 2 + m)
        for vi in (1, 3):
            h_.ts("dve", vecs[:, vi, :], vecs[:, vi, :], 1.0, None, ALU.add, r=["vecs"], w=["vecs"])

        for kc in range(2):
            h_.dma("sp", sstg[:, 0:768], w_uq[kc * 128:(kc + 1) * 128, :], w=["sstg"])
            h_.ts("dve", w_uqb[:, kc, :], sstg[:, 0:768], qg[:, kc:kc + 1], None, ALU.mult, r=["sstg", "qg"], w=["w_uqb"])
            h_.dma("sp", sstg2[:, 0:256], w_uq_sw[kc * 128:(kc + 1) * 128, :], w=["sstg2"])
            h_.memset("pool", w_uqsb[:, kc, :], 0.0, w=["w_uqsb"])
            h_.ts("dve", w_uqsb[:, kc, :].rearrange("p (h d) -> p h d", d=96)[:, :, 64:96],
                  sstg2[:, 0:256].rearrange("p (h d) -> p h d", d=32), qg[:, kc:kc + 1], None, ALU.mult,
                  r=["sstg2", "qg", "w_uqsb"], w=["w_uqsb"])
        h_.dma("sp", sstg[:, 0:1024], w_ukv, w=["sstg"])
        ukv_v = sstg[:, 0:1024].rearrange("p (h t d) -> p h t d", t=2, d=64)
        h_.ts("dve", w_kb.rearrange("p (h d) -> p h d", d=64), ukv_v[:, :, 0, :], kvg[:, 0:1], None, ALU.mult, r=["sstg", "kvg"], w=["w_kb"])
        h_.ts("dve", w_vb.rearrange("p (h d) -> p h d", d=64), ukv_v[:, :, 1, :], kvg[:, 0:1], None, ALU.mult, r=["sstg", "kvg"], w=["w_vb"])

        P.barrier()

        R1.reset(); R2.reset()
        cqnT = R2.alloc([2, S], BF16)
        ckvnT = R2.alloc([NKEY], BF16)
        KT = [R2.alloc([NKEY], BF16) for _ in range(2)]
        r2_ab = R2.off
        w_inb = R1.alloc([8, INC], BF16)
        xmT = [R1.alloc([8, 512], BF16) for _ in range(2)]
        U = [R1.alloc([4, 514], F32) for _ in range(2)]
        GB = [R1.alloc([4, 512], F32) for _ in range(2)]
        WB = R2.alloc([8, 96], BF16)
        WC = R2.alloc([8, 96], BF16)
        xt = [R2.alloc([1024], F32) for _ in range(2)]
        xn = [R2.alloc([1024], BF16) for _ in range(2)]
        junk = R2.alloc([1024], BF16)
        sq = [R2.alloc([512], BF16) for _ in range(2)]
        csbA = R2.alloc([8, 128], BF16)
        wmbA = R2.alloc([8, 512], BF16)
        bbA = R2.alloc([512], F32)
        tb = [R2.alloc([512], F32) for _ in range(2)]
        rb = [R2.alloc([512], F32) for _ in range(2)]
        gcs = [R2.alloc([512], F32) for _ in range(2)]
        ycv = [R2.alloc([512], F32) for _ in range(2)]
        convT = [R2.alloc([512], BF16) for _ in range(2)]
        cosb = [R2.alloc([512], F32) for _ in range(2)]
        sinb = [R2.alloc([512], F32) for _ in range(2)]
        rt1 = R2.alloc([512], F32)
        rt2 = R2.alloc([512], F32)
        kst = R2.alloc([8, 32], F32)

        for k in range(8):
            h_.dma("pool", w_inb[:, k, :], w_in[k * 128:(k + 1) * 128, :], w=["w_inb"])
        h_.memset("pool", WB, 0.0, w=["WB"])
        h_.memset("pool", WC, 0.0, w=["WC"])
        h_.copy("pool", WB[:, :, 64:96], w_inb[:, :, 384:416], r=["w_inb", "WB"], w=["WB"])
        h_.dma("sp", kst, w_kr_sw.rearrange("(k p) c -> p k c", p=128), w=["kst"])
        h_.copy("dve", WC[:, :, 64:96], kst, r=["kst", "WC"], w=["WC"])
        for b_ in range(2):
            for c in range(4):
                h_.memset("pool", U[b_][:, c, :], 0.0, w=["U%d_%d" % (b_, c)])
        for j in range(8):
            h_.copy("dve", csbA[:, j, :], css[:, j:j + 1].to_broadcast([128, 128]), r=["css"], w=["csbA"])

        def deferred_mod(t):
            nb = 4 + t
            m, half = nb // 2, nb % 2
            for k in range(0, 8, 2):
                h_.dma("pool", wmbA[:, k:k + 2, :], w_mod_v[:, k:k + 2, nb * 512:(nb + 1) * 512], w=["wmbA"])
            h_.dma("sp", bbA, b_mod[nb * 512:(nb + 1) * 512].partition_broadcast(128), w=["bbA"])
            zi = znext()
            h_.mms([(Z[zi], csbA[:, k, :], wmbA[:, k, :], k == 0, k == 7) for k in range(8)], r=["csbA", "wmbA"], w=[ZK[zi]])
            if m in (2, 5):
                gb_ = g1_bc if m == 2 else g2_bc
                h_.tt("dve", gb_[:, half * 512:(half + 1) * 512], Z[zi], bbA, ALU.add, r=[ZK[zi], "bbA"], w=["gbc%d_%d" % (m, half)])
            else:
                h_.tt("dve", rt1, Z[zi], bbA, ALU.add, r=[ZK[zi], "bbA"], w=["rt1"])
                for jj in range(4):
                    j = half * 4 + jj
                    h_.tt("dve", rt2[:, 0:128], rt1[:, jj * 128:(jj + 1) * 128], identf, ALU.mult, r=["rt1", "identf"], w=["rt2"])
                    h_.rsum(vecs[:, VIDX[m], j:j + 1], rt2[:, 0:128], r=["rt2"], w=["vecs2"])
            if nb == 11:
                pass

        st_ = dict(g=0, z=0, m=0, q=0, w1=0, stat=0)

        def znext():
            zi = st_["z"] % 4
            st_["z"] += 1
            return zi

        def nt_a(row):
            g = st_["g"]
            st_["g"] += 1
            s2 = g % 2
            gc_ = g % 40
            h_.dma("sp", xt[s2], row, w=["xt%d" % s2])
            h_.act(junk, xt[s2], AF.Square, r=["xt%d" % s2], w=["junk", "ssA%d" % gc_], accum=ssA[:, gc_:gc_ + 1])
            h_.act(ssB[:, gc_:gc_ + 1], ssA[:, gc_:gc_ + 1], AF.Ln, r=["ssA%d" % gc_, "epsb"], w=["ssB%d" % gc_], scale=1.0 / D, bias=epsb[:, 0:1])
            h_.act(rsA[:, gc_:gc_ + 1], ssB[:, gc_:gc_ + 1], AF.Exp, r=["ssB%d" % gc_], w=["rsA%d" % gc_], scale=-0.5)
            h_.act(xn[s2], xt[s2], AF.Copy, r=["xt%d" % s2, "rsA%d" % gc_], w=["xn%d" % s2], scale=rsA[:, gc_:gc_ + 1])
            return s2

        def nt_b(s2, i, xm, sci, shi, xmkey):
            tpb = Mps[s2].bitcast(BF16)
            h_.trs([(tpb[:, j * 128:(j + 1) * 128], xn[s2][:, j * 128:(j + 1) * 128]) for j in range(8)], idb, r=["xn%d" % s2, "idb"], w=["M%d" % s2])
            for j in range(8):
                if j % 2 == 0:
                    h_.ts("dve", xm[:, j, i * 128:(i + 1) * 128], tpb[:, j * 128:(j + 1) * 128], vecs[:, sci, j:j + 1], vecs[:, shi, j:j + 1],
                          ALU.mult, ALU.add, r=["M%d" % s2, "vecs"], w=[xmkey])
                else:
                    h_.act(xm[:, j, i * 128:(i + 1) * 128], tpb[:, j * 128:(j + 1) * 128], AF.Identity, r=["M%d" % s2, "vecs"], w=[xmkey],
                           scale=vecs[:, sci, j:j + 1], bias=vecs[:, shi, j:j + 1])

        def zmm(zi, wcol_lo, width, xm, ncol, xmkey, wt=None):
            lst = []
            for k in range(8):
                lhs = w_inb[:, k, wcol_lo:wcol_lo + width] if wt is None else wt[:, k, 0:width]
                lst.append((Z[zi][0:width, 0:ncol], lhs, xm[:, k, 0:ncol], k == 0, k == 7))
            h_.mms(lst, r=[xmkey, "w_inb", "WB", "WC"], w=[ZK[zi]])

        def rms_bc(zis, nfeat, ncol, outs, okeys, osum, okey, s2):
            n = len(zis)
            for i, zi in enumerate(zis):
                h_.act(sq[i][:, 0:ncol], Z[zi][:, 0:ncol], AF.Square, r=[ZK[zi]], w=["sq%d" % i])
            h_.mms([(osum[:, 0:ncol], ones_bf, sq[i][:, 0:ncol], i == 0, i == n - 1) for i in range(n)],
                   r=["ones_bf"] + ["sq%d" % i for i in range(n)], w=[okey])
            h_.act(tb[s2][:, 0:ncol], osum[:, 0:ncol], AF.Ln, r=[okey, "epsb"], w=["tb%d" % s2], scale=1.0 / nfeat, bias=epsb[:, 0:1])
            h_.act(rb[s2][:, 0:ncol], tb[s2][:, 0:ncol], AF.Exp, r=["tb%d" % s2], w=["rb%d" % s2], scale=-0.5)
            for i, zi in enumerate(zis):
                h_.tt("dve", outs[i], Z[zi][:, 0:ncol], rb[s2][:, 0:ncol], ALU.mult, r=[ZK[zi], "rb%d" % s2], w=[okeys[i]])

        def conv_finalize(t):
            s2 = t % 2
            for c in range(4):
                y2 = c % 2
                uk = "U%d_%d" % (s2, c)
                yk = "ycv%d" % y2
                h_.ts("dve", ycv[y2], U[s2][:, c, 1:513], convw[:, c * 3 + 1:c * 3 + 2], None, ALU.mult, r=[uk, "convw"], w=[yk])
                h_.stt(ycv[y2], U[s2][:, c, 0:512], convw[:, c * 3:c * 3 + 1], ycv[y2], ALU.mult, ALU.add, r=[uk, "convw", yk], w=[yk])
                h_.stt(ycv[y2], U[s2][:, c, 2:514], convw[:, c * 3 + 2:c * 3 + 3], ycv[y2], ALU.mult, ALU.add, r=[uk, "convw", yk], w=[yk])
                h_.tt("pool", convT[y2], GB[s2][:, c, :], ycv[y2], ALU.mult, r=["GB%d_%d" % (s2, c), yk], w=["convT%d" % y2])
                h_.dma("pool", acT[512 + c * 128:512 + (c + 1) * 128, t * 512:(t + 1) * 512], convT[y2], r=["convT%d" % y2], w=["acT_c%d" % t])

        def blk_rows(t):
            if t == 8:
                return [ctx[i * 128:(i + 1) * 128, :] for i in range(2)]
            return [x[(t * 4 + i) * 128:(t * 4 + i + 1) * 128, :] for i in range(4)]

        def nt_units(t):
            xb = t % 2
            sci, shi = (3, 2) if t == 8 else (1, 0)
            out = []
            slot = {}
            for i, row in enumerate(blk_rows(t)):
                def fa(i=i, row=row):
                    slot[i] = nt_a(row)

                def fb(i=i):
                    nt_b(slot[i], i, xmT[xb], sci, shi, "xmT%d" % xb)
                out.append((fa, fb))
            return out

        def z_units(t):
            isctx = (t == 8)
            ncol = 256 if isctx else 512
            xb = t % 2
            xmkey = "xmT%d" % xb
            k0 = t * 512
            units = []

            def u_ckv():
                if not isctx:
                    h_.dma("sp", cosb[xb][64:96, :], cos_t[:, k0:k0 + 512], w=["cosb%d" % xb])
                    h_.dma("sp", sinb[xb][64:96, :], sin_t[:, k0:k0 + 512], w=["sinb%d" % xb])
                zi = znext()
                zmm(zi, 256, 128, xmT[xb], ncol, xmkey)
                rms_bc([zi], 128, ncol, [ckvnT[:, k0:k0 + ncol]], ["ckvnT%d" % t], Ops[0], "O0", 0)
            units.append(u_ckv)

            def u_kr():
                ziB = znext()
                zmm(ziB, 0, 96, xmT[xb], ncol, xmkey, wt=WB)
                if isctx:
                    h_.copy("dve", KT[0][64:96, k0:k0 + ncol], Z[ziB][64:96, 0:ncol], r=[ZK[ziB]], w=["KTr0_%d" % t])
                else:
                    ziC = znext()
                    zmm(ziC, 0, 96, xmT[xb], ncol, xmkey, wt=WC)
                    h_.tt("dve", rt1[64:96, :], Z[ziB][64:96, :], cosb[xb][64:96, :], ALU.mult, r=[ZK[ziB], "cosb%d" % xb], w=["rt1"])
                    h_.tt("dve", rt2[64:96, :], Z[ziC][64:96, :], sinb[xb][64:96, :], ALU.mult, r=[ZK[ziC], "sinb%d" % xb], w=["rt2"])
                    h_.tt("dve", KT[0][64:96, k0:k0 + 512], rt1[64:96, :], rt2[64:96, :], ALU.add, r=["rt1", "rt2"], w=["KTr0_%d" % t])
                h_.copy("pool", KT[1][64:96, k0:k0 + ncol], KT[0][64:96, k0:k0 + ncol], r=["KTr0_%d" % t], w=["KTr1_%d" % t])
            units.append(u_kr)
            if isctx:
                return units

            def u_q():
                z0 = znext()
                zmm(z0, 0, 128, xmT[xb], ncol, xmkey)
                z1 = znext()
                zmm(z1, 128, 128, xmT[xb], ncol, xmkey)
                rms_bc([z0, z1], 256, ncol, [cqnT[:, 0, k0:k0 + 512], cqnT[:, 1, k0:k0 + 512]], ["cqnT%d" % t, "cqnT%d" % t], Ops[1], "O1", 1)
            units.append(u_q)
            pb = 1 - xb

            def u_conv(c):
                g2 = c % 2
                uk = "U%d_%d" % (xb, c)
                pk = "U%d_%d" % (pb, c)
                zb = znext()
                zmm(zb, 416 + c * 128, 128, xmT[xb], ncol, xmkey)
                h_.copy("act", GB[xb][:, c, :], Z[zb], r=[ZK[zb]], w=["GB%d_%d" % (xb, c)])
                zc = znext()
                zmm(zc, 928 + c * 128, 128, xmT[xb], ncol, xmkey)
                h_.copy("act", gcs[g2], Z[zc], r=[ZK[zc]], w=["gcs%d" % g2])
                zx = znext()
                zmm(zx, 1440 + c * 128, 128, xmT[xb], ncol, xmkey)
                h_.tt("dve", U[xb][:, c, 1:513], Z[zx], gcs[g2], ALU.mult, r=[ZK[zx], "gcs%d" % g2], w=[uk])
                if t > 0:
                    h_.copy("pool", U[xb][:, c, 0:1], U[pb][:, c, 512:513], r=[pk, uk], w=[uk])
                    h_.copy("pool", U[pb][:, c, 513:514], U[xb][:, c, 1:2], r=[uk, pk], w=[pk])
                else:
                    h_.memset("pool", U[xb][:, c, 0:1], 0.0, r=[uk], w=[uk])
            for c in range(4):
                units.append(lambda c=c: u_conv(c))

            def u_fin():
                if t > 0:
                    conv_finalize(t - 1)
                if t == 7:
                    for c in range(4):
                        h_.memset("pool", U[xb][:, c, 513:514], 0.0, r=["U%d_%d" % (xb, c)], w=["U%d_%d" % (xb, c)])
                    conv_finalize(7)
            units.append(u_fin)
            units.append(lambda: deferred_mod(t))
            return units

        for fa, fb in nt_units(0):
            fa()
            fb()
        for t in range(9):
            nxt = nt_units(t + 1) if t + 1 < 9 else []
            na = 0
            nb_ = 0
            for idx, u in enumerate(z_units(t)):
                u()
                if idx >= 2 and nb_ < na:
                    nxt[nb_][1]()
                    nb_ += 1
                if na < len(nxt):
                    nxt[na][0]()
                    na += 1
            while nb_ < len(nxt):
                if na <= nb_:
                    nxt[na][0]()
                    na += 1
                nxt[nb_][1]()
                nb_ += 1
                if na < len(nxt):
                    nxt[na][0]()
                    na += 1

        h_.ts("dve", vecs[:, 5, :], vecs[:, 5, :], 1.0, None, ALU.add, r=["vecs2"], w=["vecs2"])

        P.barrier()

        R1.reset(); R2.reset(r2_ab)
        w_outb = R1.alloc([8, 1024], BF16)
        w2b = R1.alloc([32, 1024], BF16)
        V = [R2.alloc([NKT, 65], BF16) for _ in range(2)]
        qT = [R2.alloc([S], BF16) for _ in range(2)]
        PT = [R2.alloc([1024], BF16) for _ in range(3)]
        qcos = [R2.alloc([512], F32) for _ in range(2)]
        qsin = [R2.alloc([512], F32) for _ in range(2)]
        qt1 = R2.alloc([512], F32)
        qt2 = R2.alloc([512], F32)
        den = R2.alloc([512], F32)
        rec = R2.alloc([512], F32)
        aoT = [R2.alloc([512], BF16) for _ in range(2)]
        bstg = [R2.alloc([1024], F32) for _ in range(2)]

        for b_ in range(2):
            h_.memset("pool", V[b_][:, :, 64:65], 1.0, w=["Vones%d" % b_])

        wq = [("o", j) for j in range(8)] + [("2", j) for j in range(32)]

        w1tmp = w2b[:, 28:32, :]
        w1s_w = w1s.rearrange("f p (k c) -> p f k c", k=8)

        def precast_w1(k):
            h_.dma("pool", w1tmp, w1[k * 128:(k + 1) * 128, :].rearrange("p (a b) -> p a b", b=1024), w=["w1tmp"])
            h_.dma("sp", w1s_w[:, :, k, :], w1tmp.rearrange("p a (g c) -> p (a g) c", c=128), r=["w1tmp"], w=["w1s"])

        def load_some_weights(n):
            for _ in range(n):
                if st_["q"] >= len(wq):
                    return
                kind, j = wq[st_["q"]]
                s2 = st_["q"] % 2
                st_["q"] += 1
                if kind == "o":
                    h_.dma("sp", bstg[s2], w_out[j * 128:(j + 1) * 128, :], w=["bstg%d" % s2])
                    h_.tt("pool", w_outb[:, j, :], bstg[s2], g1_bc, ALU.mult, r=["bstg%d" % s2], w=["w_outb"])
                else:
                    h_.dma("sp", bstg[s2], w2[j * 128:(j + 1) * 128, :], w=["bstg%d" % s2])
                    h_.tt("pool", w2b[:, j, :], bstg[s2], g2_bc, ALU.mult, r=["bstg%d" % s2], w=["w2b"] + (["w1tmp"] if j >= 28 else []))

        def mnext():
            m = st_["m"] % 2
            st_["m"] += 1
            return m

        NP = NKT // 2
        qdc = [0]

        def setup_units(h):
            hb = h % 2
            units = []

            def kunit(kb):
                n = 512 if kb < 8 else 256
                m = mnext()
                h_.mms([(Mps[m][0:64, 0:n], w_kb[:, h * 64:(h + 1) * 64], ckvnT[:, kb * 512:kb * 512 + n], True, True)], r=["w_kb", "ckvnT%d" % kb], w=["M%d" % m])
                h_.copy("dve", KT[hb][0:64, kb * 512:kb * 512 + n], Mps[m][0:64, 0:n], r=["M%d" % m], w=["KTn%d_%d" % (hb, kb)])

            def vunit(g0):
                ng = min(8, NKT - g0)
                m = mnext()
                h_.mms([(Mps[m][:, jj * 64:(jj + 1) * 64], ckvnT[:, (g0 + jj) * 128:(g0 + jj + 1) * 128], w_vb[:, h * 64:(h + 1) * 64], True, True) for jj in range(ng)],
                       r=["w_vb"] + ["ckvnT%d" % kb for kb in range(9)], w=["M%d" % m])
                h_.copy("dve", V[hb][:, g0:g0 + ng, 0:64], Mps[m][:, 0:ng * 64].rearrange("p (a b) -> p a b", b=64), r=["M%d" % m], w=["V%d" % hb])

            def qunit(qb):
                q0 = qb * 512
                cs2 = qdc[0] % 2
                qdc[0] += 1
                h_.dma("sp", qcos[cs2][64:96, :], cos_t[:, q0:q0 + 512], w=["qcos%d" % cs2])
                h_.dma("sp", qsin[cs2][64:96, :], sin_t[:, q0:q0 + 512], w=["qsin%d" % cs2])
                ma = mnext()
                mb = mnext()
                h_.mms([(Mps[ma][0:96, :], w_uqb[:, kc, h * 96:(h + 1) * 96], cqnT[:, kc, q0:q0 + 512], kc == 0, kc == 1) for kc in range(2)], r=["w_uqb", "cqnT%d" % qb], w=["M%d" % ma])
                h_.mms([(Mps[mb][0:96, :], w_uqsb[:, kc, h * 96:(h + 1) * 96], cqnT[:, kc, q0:q0 + 512], kc == 0, kc == 1) for kc in range(2)], r=["w_uqsb", "cqnT%d" % qb], w=["M%d" % mb])
                qk_ = "qT%d_%d" % (hb, qb)
                h_.copy("dve", qT[hb][0:64, q0:q0 + 512], Mps[ma][0:64, :], r=["M%d" % ma], w=[qk_])
                h_.tt("dve", qt1[64:96, :], Mps[ma][64:96, :], qcos[cs2][64:96, :], ALU.mult, r=["M%d" % ma, "qcos%d" % cs2], w=["qt1"])
                h_.tt("dve", qt2[64:96, :], Mps[mb][64:96, :], qsin[cs2][64:96, :], ALU.mult, r=["M%d" % mb, "qsin%d" % cs2], w=["qt2"])
                h_.tt("dve", qT[hb][64:96, q0:q0 + 512], qt1[64:96, :], qt2[64:96, :], ALU.add, r=["qt1", "qt2"], w=[qk_])

            for kb in range(9):
                units.append(lambda kb=kb: kunit(kb))
            for g0 in range(0, NKT, 8):
                units.append(lambda g0=g0: vunit(g0))
            for qb in range(8):
                units.append(lambda qb=qb: qunit(qb))
            return units

        osb = R2.alloc([512], F32)
        steps = [(h, qb, jp) for h in range(NH) for qb in range(8) for jp in range(NP)]
        NS_ = len(steps)

        def qk(s):
            h, qb, jp = steps[s]
            hb = h % 2
            q0 = qb * 512
            sp_ = s % 2
            lst = []
            keys = ["qT%d_%d" % (hb, qb)]
            for u in range(2):
                j = 2 * jp + u
                kb = (j * 128) // 512
                keys += ["KTn%d_%d" % (hb, kb), "KTr%d_%d" % (hb, kb)]
                lst.append((Sps[sp_][:, u * 512:(u + 1) * 512], KT[hb][0:96, j * 128:(j + 1) * 128], qT[hb][0:96, q0:q0 + 512], True, True))
            h_.mms(lst, r=keys, w=["S%d" % sp_])

        def ex(s):
            sp_ = s % 2
            pp = s % 3
            h_.act(PT[pp], Sps[sp_], AF.Exp, r=["S%d" % sp_], w=["PT%d" % pp], scale=ATTN_SCALE)

        def pv(s):
            h, qb, jp = steps[s]
            hb = h % 2
            ob = qb % 2
            pp = s % 3
            lst = []
            for u in range(2):
                j = 2 * jp + u
                lst.append((Ops[ob][0:65, :], V[hb][:, j, 0:65], PT[pp][:, u * 512:(u + 1) * 512], j == 0, j == NKT - 1))
            h_.mms(lst, r=["V%d" % hb, "Vones%d" % hb, "PT%d" % pp], w=["O%d" % ob])

        def norm1(h, qb):
            ob = qb % 2
            h_.copy("dve", den[64:65, :], Ops[ob][64:65, :], r=["O%d" % ob], w=["den"])
            h_.copy("dve", osb[0:64, :], Ops[ob][0:64, :], r=["O%d" % ob], w=["osb"])

        def norm2(h, qb):
            m = mnext()
            h_.mms([(Mps[m][0:64, :], onesf[64:65, 0:64], den[64:65, :], True, True)], r=["den", "onesf"], w=["M%d" % m])
            h_.recip(rec[0:64, :], Mps[m][0:64, :], r=["M%d" % m], w=["rec"])
            a2 = (h * 8 + qb) % 2
            h_.tt("dve", aoT[a2][0:64, :], osb[0:64, :], rec[0:64, :], ALU.mult, r=["osb", "rec"], w=["aoT%d" % a2])
            h_.dma("pool", acT[h * 64:(h + 1) * 64, qb * 512:(qb + 1) * 512], aoT[a2][0:64, :], r=["aoT%d" % a2], w=["acT_a%d_%d" % (h, qb)])

        for u in setup_units(0):
            u()
        qk(0)
        ex(0)
        qk(1)
        ex(1)
        pending = None
        nxt = []
        cnt = 0
        for s in range(NS_):
            h, qb, jp = steps[s]
            if jp == 0 and qb == 0:
                if h < 4:
                    precast_w1(2 * h)
                    precast_w1(2 * h + 1)
                load_some_weights(5)
                nxt = setup_units(h + 1) if h + 1 < NH else []
                cnt = 0
            if s + 2 < NS_:
                if steps[s + 2][0] != h:
                    while nxt:
                        nxt.pop(0)()
                qk(s + 2)
            pv(s)
            if s + 2 < NS_:
                ex(s + 2)
            if pending is not None and pending[0] == s:
                norm2(pending[1], pending[2])
                pending = None
            if jp == NP - 1:
                norm1(h, qb)
                pending = (s + 2, h, qb)
            cnt += 1
            if nxt and cnt % 5 == 3:
                nxt.pop(0)()
        if pending is not None:
            norm2(pending[1], pending[2])
        load_some_weights(100)

        P.barrier()

        R2.reset()
        xs = [[R2.alloc([1024], F32) for _ in range(4)] for _ in range(2)]
        acb = R2.alloc([8, 512], BF16)
        xn2 = [R2.alloc([1024], BF16) for _ in range(4)]
        xm2T = R2.alloc([8, 512], BF16)
        hT = R2.alloc([32, 512], BF16)
        hr = [R2.alloc([512], F32) for _ in range(2)]
        junk2 = hr[0].bitcast(BF16)
        w1c = [R2.alloc([8, 128], BF16) for _ in range(5)]
        w1_v = w1.rearrange("(k p) n -> p k n", p=128)

        def stats(src, srckey):
            g = st_["stat"] % 40
            st_["stat"] += 1
            h_.act(junk2, src, AF.Square, r=[srckey], w=["hr0", "ssA%d" % g], accum=ssA[:, g:g + 1])
            h_.act(ssB[:, g:g + 1], ssA[:, g:g + 1], AF.Ln, r=["ssA%d" % g, "epsb"], w=["ssB%d" % g], scale=1.0 / D, bias=epsb[:, 0:1])
            h_.act(rsA[:, g:g + 1], ssB[:, g:g + 1], AF.Exp, r=["ssB%d" % g], w=["rsA%d" % g], scale=-0.5)
            return g

        def load_block(t):
            for i in range(4):
                h_.dma("sp", xs[t % 2][i], x[t * 512 + i * 128:t * 512 + (i + 1) * 128, :], w=["x%d_%d" % (t % 2, i)])

        def load_acb(t):
            for j in range(8):
                h_.dma("sp", acb[:, j, :], acT[j * 128:(j + 1) * 128, t * 512:(t + 1) * 512], w=["acb"])

        def outproj_stats(t):
            X = xs[t % 2]
            for i in range(4):
                xk = "x%d_%d" % (t % 2, i)
                for hf in range(2):
                    zi = znext()
                    h_.mms([(Z[zi], acb[:, j, i * 128:(i + 1) * 128], w_outb[:, j, hf * 512:(hf + 1) * 512], j == 0, j == 7) for j in range(8)], r=["acb", "w_outb"], w=[ZK[zi]])
                    h_.tt("dve", X[i][:, hf * 512:(hf + 1) * 512], Z[zi], X[i][:, hf * 512:(hf + 1) * 512], ALU.add, r=[ZK[zi], xk], w=[xk])
                g = stats(X[i], xk)
                h_.act(xn2[i], X[i], AF.Copy, r=[xk, "rsA%d" % g], w=["xn2_%d" % i], scale=rsA[:, g:g + 1])

        def trans(i):
            s2 = i % 2
            tpb = Mps[s2].bitcast(BF16)
            h_.trs([(tpb[:, j * 128:(j + 1) * 128], xn2[i][:, j * 128:(j + 1) * 128]) for j in range(8)], idb, r=["xn2_%d" % i, "idb"], w=["M%d" % s2])
            for j in range(8):
                if j % 2 == 0:
                    h_.ts("dve", xm2T[:, j, i * 128:(i + 1) * 128], tpb[:, j * 128:(j + 1) * 128], vecs[:, 5, j:j + 1], vecs[:, 4, j:j + 1],
                          ALU.mult, ALU.add, r=["M%d" % s2, "vecs"], w=["xm2T"])
                else:
                    h_.act(xm2T[:, j, i * 128:(i + 1) * 128], tpb[:, j * 128:(j + 1) * 128], AF.Identity, r=["M%d" % s2, "vecs"], w=["xm2T"],
                           scale=vecs[:, 5, j:j + 1], bias=vecs[:, 4, j:j + 1])

        def mlp1(t):
            for f_ in range(32):
                n = st_["w1"]
                st_["w1"] += 1
                c4 = n % 5
                h_.dma("sp", w1c[c4], w1s[f_].rearrange("p (k c) -> p k c", k=8), w=["w1c%d" % c4])
                zi = znext()
                h_.mms([(Z[zi], w1c[c4][:, k, :], xm2T[:, k, :], k == 0, k == 7) for k in range(8)], r=["w1c%d" % c4, "xm2T"], w=[ZK[zi]])
                h2 = f_ % 2
                h_.act(hr[h2], Z[zi], AF.Relu, r=[ZK[zi]], w=["hr%d" % h2])
                h_.tt("dve", hT[:, f_, :], hr[h2], hr[h2], ALU.mult, r=["hr%d" % h2], w=["hT%d" % f_])

        def mlp2_final(t, with_trans):
            X = xs[t % 2]
            for i in range(4):
                xk = "x%d_%d" % (t % 2, i)
                for hf in range(2):
                    zi = znext()
                    h_.mms([(Z[zi], hT[:, k, i * 128:(i + 1) * 128], w2b[:, k, hf * 512:(hf + 1) * 512], k == 0, k == 31) for k in range(32)],
                           r=["w2b"] + ["hT%d" % k for k in range(32)], w=[ZK[zi]])
                    h_.tt("dve", X[i][:, hf * 512:(hf + 1) * 512], Z[zi], X[i][:, hf * 512:(hf + 1) * 512], ALU.add, r=[ZK[zi], xk], w=[xk])
                if with_trans:
                    trans(i)
                g = stats(X[i], xk)
                h_.stt(X[i], X[i], rsA[:, g:g + 1], gf_bc, ALU.mult, ALU.mult, r=[xk, "rsA%d" % g, "gf_bc"], w=[xk])
                h_.dma("sp", y[t * 512 + i * 128:t * 512 + (i + 1) * 128, :], X[i], r=[xk], w=["y%d_%d" % (t, i)])

        load_block(0)
        load_acb(0)
        load_block(1)
        outproj_stats(0)
        load_acb(1)
        for i in range(4):
            trans(i)
        for t in range(8):
            mlp1(t)
            if t + 1 < 8:
                outproj_stats(t + 1)
                if t + 2 < 8:
                    load_acb(t + 2)
            mlp2_final(t, t + 1 < 8)
            if t + 2 < 8:
                load_block(t + 2)

        P.emit(nc, block, st)
    return nc


def _host_inputs(x, c, ctx, c_ctx, w_mod, b_mod, w_in, q_norm_g, w_uq, kv_norm_g, w_ukv,
                 conv_w, w_out, w_mlp1, w_mlp2, final_norm_g):
    f = lambda a: np.ascontiguousarray(np.asarray(a), dtype=np.float32)
    x, c, ctx, c_ctx = f(x), f(c), f(ctx), f(c_ctx)
    w_mod, b_mod, w_in = f(w_mod)[0], f(b_mod)[0], f(w_in)[0]
    q_norm_g, w_uq, kv_norm_g, w_ukv = f(q_norm_g)[0], f(w_uq)[0], f(kv_norm_g)[0], f(w_ukv)[0]
    conv_w, w_out, w_mlp1, w_mlp2, gfin = f(conv_w)[0], f(w_out)[0], f(w_mlp1)[0], f(w_mlp2)[0], f(final_norm_g)
    perm = np.array(list(range(8, 16)) + list(range(0, 8)) + list(range(24, 32)) + list(range(16, 24)))
    w_kr_sw = f(w_in[:, 384:416][:, perm])
    w_uq_sw = f(w_uq.reshape(256, 8, 96)[:, :, 64:96][:, :, perm].reshape(256, 256))
    pos = np.arange(S)
    row = (pos // 64).astype(np.float32)
    col = (pos % 64).astype(np.float32)
    freqs = (np.float32(10000.0) ** (-np.arange(0, 16, 2, dtype=np.float32) / np.float32(16))).astype(np.float32)
    ar = (row[None, :] * freqs[:, None]).astype(np.float32)
    ac = (col[None, :] * freqs[:, None]).astype(np.float32)
    cos_t = np.concatenate([np.cos(ar), np.cos(ar), np.cos(ac), np.cos(ac)], 0).astype(np.float32)
    sin_t = np.concatenate([-np.sin(ar), np.sin(ar), -np.sin(ac), np.sin(ac)], 0).astype(np.float32)
    shared = dict(
        cc_fm=f(c_ctx.reshape(8, 128).T), w_mod=w_mod, b_mod=b_mod, w_in=w_in, w_kr_sw=w_kr_sw,
        qg_fm=f(q_norm_g.reshape(2, 128).T), w_uq=w_uq, w_uq_sw=w_uq_sw, kvg_fm=f(kv_norm_g.reshape(1, 128).T),
        w_ukv=w_ukv, convw_fm=f(conv_w.reshape(3, 4, 128).transpose(2, 1, 0).reshape(128, 12)),
        w_out=w_out, w_mlp1=w_mlp1, w_mlp2=w_mlp2, gf=gfin, ident=np.eye(128, dtype=np.float32),
        cos_t=cos_t, sin_t=sin_t)
    maps = []
    for b in range(8):
        m = dict(shared)
        m["x"] = x[b]
        m["ctx"] = ctx[b]
        m["c_fm"] = f(c[b].reshape(8, 128).T)
        maps.append(m)
    return maps


_NC = None


def kernel(**inputs):
    global _NC
    maps = _host_inputs(**inputs)
    if _NC is None:
        _NC = build()
    res = run_bass_kernel_spmd(_NC, maps, core_ids=list(range(8)))
    return np.stack([np.asarray(r["y"]) for r in res.results], 0).astype(np.float32)
```

```python
import math
import numpy as np
from contextlib import ExitStack
import concourse.bass as bass
import concourse.mybir as mybir
from concourse.bass_utils import run_bass_kernel_spmd

F32 = mybir.dt.float32
BF16 = mybir.dt.bfloat16
ALU = mybir.AluOpType
AF = mybir.ActivationFunctionType

D = 1024
S = 4096
L = 256
NH = 8
NKEY = S + L
NKT = NKEY // 128
INC = 1952
EPS = 1e-6
ATTN_SCALE = 1.0 / math.sqrt(96.0)

ENGS = ["pe", "act", "dve", "pool", "sp"]
N_DMA_SEMS = {"sp": 12, "pool": 10}


class Prog:
    def __init__(self):
        self.ops = []
        self.lw = {}
        self.rd = {}
        self.eng_ops = {e: [] for e in ENGS}
        self.ndma = {e: 0 for e in ENGS}
        self.pending = {}
        self.last_op = {}
        self.last_dma = {}

    def op(self, eng, fn, r=(), w=(), dma=False):
        i = len(self.ops)
        deps = set()
        for k in r:
            if k in self.lw:
                deps.add(self.lw[k])
        for k in w:
            if k in self.lw:
                deps.add(self.lw[k])
            deps.update(self.rd.get(k, ()))
        if eng in self.pending:
            deps |= self.pending.pop(eng)
        for k in r:
            self.rd.setdefault(k, []).append(i)
        for k in w:
            self.lw[k] = i
            self.rd[k] = []
        o = dict(eng=eng, fn=fn, deps=deps, dma=dma, sig=False)
        if dma:
            k = self.ndma[eng]
            o["dma_k"] = k
            self.ndma[eng] += 1
            self.last_dma[(eng, k % N_DMA_SEMS[eng])] = i
        else:
            self.last_op[eng] = i
        self.ops.append(o)
        self.eng_ops[eng].append(i)
        return i

    def dma(self, eng, out, in_, r=(), w=()):
        return self.op(eng, lambda e: e.dma_start(out=out, in_=in_), r=r, w=w, dma=True)

    def barrier(self):
        deps = set(self.last_op.values()) | set(self.last_dma.values())
        for e in ENGS:
            self.pending[e] = set(deps) | self.pending.get(e, set())

    def emit(self, nc, block, semstack):
        ops = self.ops
        engsem = {e: semstack.enter_context(nc.semaphore("s_" + e)) for e in ENGS}
        dmasem = {
            e: [semstack.enter_context(nc.semaphore("d_%s%d" % (e, i))) for i in range(n)]
            for e, n in N_DMA_SEMS.items()
        }
        for o in ops:
            for d in o["deps"]:
                dd = ops[d]
                if (not dd["dma"]) and dd["eng"] == o["eng"] and o["eng"] == "pe":
                    continue
                dd["sig"] = True
        cnt = {e: 0 for e in ENGS}
        for o in ops:
            if o["dma"]:
                n = N_DMA_SEMS[o["eng"]]
                o["sem"] = dmasem[o["eng"]][o["dma_k"] % n]
                o["val"] = 16 * (o["dma_k"] // n + 1)
            elif o["sig"]:
                cnt[o["eng"]] += 1
                o["sem"] = engsem[o["eng"]]
                o["val"] = cnt[o["eng"]]

        def run(engname, e, final=False):
            known = {}
            for i in self.eng_ops[engname]:
                o = ops[i]
                waits = {}
                for d in o["deps"]:
                    dd = ops[d]
                    if not dd["sig"] and not dd["dma"]:
                        continue
                    s, v = dd["sem"], dd["val"]
                    key = id(s)
                    if key not in waits or waits[key][1] < v:
                        waits[key] = (s, v)
                if o["dma"] and o["val"] > 16:
                    s = o["sem"]
                    key = id(s)
                    v = o["val"] - 16
                    if key not in waits or waits[key][1] < v:
                        waits[key] = (s, v)
                for key, (s, v) in waits.items():
                    if known.get(key, 0) < v:
                        e.wait_ge(s, v)
                        known[key] = v
                ins = o["fn"](e)
                if o["dma"]:
                    ins.then_inc(o["sem"], 16)
                elif o["sig"]:
                    ins.then_inc(o["sem"], 1)
            if final:
                last = {}
                for o in ops:
                    if o["dma"]:
                        key = id(o["sem"])
                        if key not in last or last[key][1] < o["val"]:
                            last[key] = (o["sem"], o["val"])
                for key, (s, v) in last.items():
                    if known.get(key, 0) < v:
                        e.wait_ge(s, v)

        @block.tensor
        def _(e):
            run("pe", e)

        @block.scalar
        def _(e):
            run("act", e)

        @block.vector
        def _(e):
            run("dve", e)

        @block.gpsimd
        def _(e):
            run("pool", e)

        @block.sync
        def _(e):
            run("sp", e, final=True)


class Arena:
    def __init__(self, t, base, cap):
        self.t = t
        self.base = base
        self.cap = cap
        self.off = 0

    def reset(self, off=0):
        self.off = off

    def alloc(self, shape, dt):
        esz = 4 if dt == F32 else 2
        ne = int(np.prod(shape))
        n = (ne * esz + 31) // 32 * 32
        o = self.base + self.off
        assert self.off + n <= self.cap, ("arena overflow", self.off, n, self.cap)
        self.off += n
        v = self.t[:, o // 4:(o + n) // 4]
        if dt != F32:
            v = v.bitcast(dt)
        v = v[:, 0:ne]
        if len(shape) == 2:
            v = v.rearrange("p (a b) -> p a b", b=shape[1])
        elif len(shape) == 3:
            v = v.rearrange("p (a b c) -> p a b c", b=shape[1], c=shape[2])
        return v


class H:
    def __init__(self, P):
        self.P = P
        self.rr = 0

    def dma(self, eng, out, in_, r=(), w=()):
        return self.P.op(eng, lambda e: e.dma_start(out=out, in_=in_), r=r, w=w, dma=True)

    def copy(self, eng, out, in_, r=(), w=()):
        if eng == "act":
            return self.P.op("act", lambda e: e.activation(out=out, in_=in_, func=AF.Copy), r=r, w=w)
        return self.P.op(eng, lambda e: e.tensor_copy(out, in_), r=r, w=w)

    def cast(self, out, in_, r=(), w=(), engs=("dve", "pool", "act")):
        e = engs[self.rr % len(engs)]
        self.rr += 1
        return self.copy(e, out, in_, r, w)

    def memset(self, eng, ap, val, r=(), w=()):
        return self.P.op(eng, lambda e: e.memset(ap, val), r=r, w=w)

    def tt(self, eng, out, in0, in1, op, r=(), w=()):
        return self.P.op(eng, lambda e: e.tensor_tensor(out=out, in0=in0, in1=in1, op=op), r=r, w=w)

    def ts(self, eng, out, in0, s1, s2, op0, op1=None, r=(), w=()):
        if op1 is None:
            return self.P.op(eng, lambda e: e.tensor_scalar(out=out, in0=in0, scalar1=s1, scalar2=None, op0=op0), r=r, w=w)
        return self.P.op(eng, lambda e: e.tensor_scalar(out=out, in0=in0, scalar1=s1, scalar2=s2, op0=op0, op1=op1), r=r, w=w)

    def stt(self, out, in0, scalar, in1, op0, op1, r=(), w=()):
        return self.P.op("dve", lambda e: e.scalar_tensor_tensor(out=out, in0=in0, scalar=scalar, in1=in1, op0=op0, op1=op1), r=r, w=w)

    def act(self, out, in_, func, r=(), w=(), scale=None, accum=None, bias=None):
        kw = {}
        if bias is not None:
            kw["bias"] = bias
        if scale is not None:
            kw["scale"] = scale
        if accum is not None:
            kw["accum_out"] = accum
        return self.P.op("act", lambda e: e.activation(out=out, in_=in_, func=func, **kw), r=r, w=w)

    def recip(self, out, in_, r=(), w=()):
        return self.P.op("dve", lambda e: e.reciprocal(out=out, in_=in_), r=r, w=w)

    def rsum(self, out, in_, r=(), w=()):
        return self.P.op("dve", lambda e: e.reduce_sum(out=out, in_=in_, axis=mybir.AxisListType.X), r=r, w=w)

    def mms(self, lst, r=(), w=()):
        lst = list(lst)

        def f(e):
            for (o, l, rh, s0, s1) in lst:
                ins = e.matmul(o, lhsT=l, rhs=rh, start=s0, stop=s1)
            return ins
        return self.P.op("pe", f, r=r, w=w)

    def trs(self, lst, ident, r=(), w=()):
        lst = list(lst)

        def f(e):
            for (o, i) in lst:
                ins = e.transpose(out=o, in_=i, identity=ident)
            return ins
        return self.P.op("pe", f, r=r, w=w)


P_BYTES = 23 * 1024
R1_BYTES = 80 * 1024
TOTAL_BYTES = 212800


def build(dbg=False):
    nc = bass.Bass("TRN2", target_bir_lowering=False)

    def din(name, shape, dt=F32):
        return nc.dram_tensor(name, shape, dt, kind="ExternalInput").ap()

    x = din("x", [S, D])
    ctx = din("ctx", [L, D])
    c_fm = din("c_fm", [128, 8])
    cc_fm = din("cc_fm", [128, 8])
    w_mod = din("w_mod", [D, 6 * D])
    b_mod = din("b_mod", [6 * D])
    w_in = din("w_in", [D, INC])
    w_kr_sw = din("w_kr_sw", [D, 32])
    qg_fm = din("qg_fm", [128, 2])
    w_uq = din("w_uq", [256, 768])
    w_uq_sw = din("w_uq_sw", [256, 256])
    kvg_fm = din("kvg_fm", [128, 1])
    w_ukv = din("w_ukv", [128, 1024])
    convw_fm = din("convw_fm", [128, 12])
    w_out = din("w_out", [D, D])
    w1 = din("w_mlp1", [D, 4096])
    w2 = din("w_mlp2", [4096, D])
    gf = din("gf", [D])
    ident = din("ident", [128, 128])
    cos_t = din("cos_t", [32, S])
    sin_t = din("sin_t", [32, S])
    y = nc.dram_tensor("y", [S, D], F32, kind="ExternalOutput").ap()
    acT = nc.dram_tensor("acT", [D, S], BF16, kind="Internal").ap()
    w1s = nc.dram_tensor("w1s", [32, 128, 1024], BF16, kind="Internal").ap()

    P = Prog()
    h_ = H(P)
    with ExitStack() as st:
        arena_t = st.enter_context(nc.sbuf_tensor("arena", [128, TOTAL_BYTES // 4], F32))
        AP_ = Arena(arena_t, 0, P_BYTES)
        R1 = Arena(arena_t, P_BYTES, R1_BYTES)
        R2 = Arena(arena_t, P_BYTES + R1_BYTES, TOTAL_BYTES - P_BYTES - R1_BYTES)
        SA = st.enter_context(nc.psum_tensor("SA", [128, 1024], F32))
        SB = st.enter_context(nc.psum_tensor("SB", [128, 1024], F32))
        O0 = st.enter_context(nc.psum_tensor("O0", [128, 512], F32))
        O1 = st.enter_context(nc.psum_tensor("O1", [128, 512], F32))
        M0 = st.enter_context(nc.psum_tensor("M0", [128, 512], F32))
        M1 = st.enter_context(nc.psum_tensor("M1", [128, 512], F32))
        block = st.enter_context(nc.Block())
        Sps = [SA[:, :], SB[:, :]]
        Ops = [O0[:, :], O1[:, :]]
        Mps = [M0[:, :], M1[:, :]]
        Z = [SA[:, 0:512], SA[:, 512:1024], SB[:, 0:512], SB[:, 512:1024]]
        ZK = ["Z0", "Z1", "Z2", "Z3"]

        idb = AP_.alloc([128], BF16)
        identf = AP_.alloc([128], F32)
        ones_bf = AP_.alloc([128], BF16)
        onesf = AP_.alloc([64], F32)
        mh = AP_.alloc([1], F32)
        epsb = AP_.alloc([1], F32)
        css = AP_.alloc([8], F32)
        vecs = AP_.alloc([8, 8], F32)
        qg = AP_.alloc([2], F32)
        kvg = AP_.alloc([1], F32)
        convw = AP_.alloc([12], F32)
        ssA = AP_.alloc([40], F32)
        ssB = AP_.alloc([40], F32)
        rsA = AP_.alloc([40], F32)
        gf_bc = AP_.alloc([1024], F32)
        g1_bc = AP_.alloc([1024], F32)
        g2_bc = AP_.alloc([1024], F32)
        w_uqb = AP_.alloc([2, 768], BF16)
        w_uqsb = AP_.alloc([2, 768], BF16)
        w_kb = AP_.alloc([512], BF16)
        w_vb = AP_.alloc([512], BF16)

        h_.dma("sp", identf, ident, w=["identf"])
        h_.copy("dve", idb, identf, r=["identf"], w=["idb"])
        h_.memset("pool", ones_bf, 1.0, w=["ones_bf"])
        h_.memset("pool", onesf, 1.0, w=["onesf"])
        h_.memset("pool", mh, -0.5, w=["mh"])
        h_.memset("pool", epsb, EPS, w=["epsb"])
        h_.dma("sp", qg, qg_fm, w=["qg"])
        h_.dma("sp", kvg, kvg_fm, w=["kvg"])
        h_.dma("sp", convw, convw_fm, w=["convw"])
        h_.dma("sp", gf_bc, gf.partition_broadcast(128), w=["gf_bc"])

        R1.reset(); R2.reset()
        cs = R2.alloc([8], F32)
        ccs = R2.alloc([8], F32)
        ccss = R2.alloc([8], F32)
        csb = R2.alloc([8, 128], BF16)
        ccb = R2.alloc([8, 128], BF16)
        bb = [R2.alloc([512], F32) for _ in range(3)]
        modbc = [R2.alloc([512], F32) for _ in range(3)]
        junkf = R2.alloc([128], F32)
        wmb = [R1.alloc([8, 512], BF16) for _ in range(3)]
        sstg = R2.alloc([2048], F32)
        sstg2 = R2.alloc([1024], F32)

        h_.dma("sp", cs, c_fm, w=["cs"])
        h_.dma("sp", ccs, cc_fm, w=["ccs"])
        h_.act(css, cs, AF.Silu, r=["cs"], w=["css"])
        h_.act(ccss, ccs, AF.Silu, r=["ccs"], w=["ccss"])
        for j in range(8):
            h_.copy("dve", csb[:, j, :], css[:, j:j + 1].to_broadcast([128, 128]), r=["css"], w=["csb"])
            h_.copy("dve", ccb[:, j, :], ccss[:, j:j + 1].to_broadcast([128, 128]), r=["ccss"], w=["ccb"])

        w_mod_v = w_mod.rearrange("(k p) n -> p k n", p=128)
        VIDX = {0: 0, 1: 1, 3: 4, 4: 5}

        def diag_extract(sl, half, vi):
            for jj in range(4):
                j = half * 4 + jj
                h_.tt("dve", junkf, modbc[sl][:, jj * 128:(jj + 1) * 128], identf, ALU.mult, r=["modbc%d" % sl, "identf"], w=["junkf"])
                h_.rsum(vecs[:, vi, j:j + 1], junkf, r=["junkf"], w=["vecs"])

        for nb in range(12):
            sl = nb % 3
            m, half = nb // 2, nb % 2
            for k in range(0, 8, 2):
                h_.dma("pool", wmb[sl][:, k:k + 2, :], w_mod_v[:, k:k + 2, nb * 512:(nb + 1) * 512], w=["wmb%d_%d" % (sl, kk) for kk in range(k, k + 2)])
            h_.dma("sp", bb[sl], b_mod[nb * 512:(nb + 1) * 512].partition_broadcast(128), w=["bb%d" % sl])
            zi = nb % 2
            h_.mms([(Z[zi], csb[:, k, :], wmb[sl][:, k, :], k == 0, k == 7) for k in range(8)],
                   r=["csb"] + ["wmb%d_%d" % (sl, k) for k in range(8)], w=[ZK[zi]])
            if m in (2, 5):
                gb_ = g1_bc if m == 2 else g2_bc
                h_.tt("dve", gb_[:, half * 512:(half + 1) * 512], Z[zi], bb[sl], ALU.add, r=[ZK[zi], "bb%d" % sl], w=["gbc%d_%d" % (m, half)])
            else:
                h_.tt("dve", modbc[sl], Z[zi], bb[sl], ALU.add, r=[ZK[zi], "bb%d" % sl], w=["modbc%d" % sl])
                diag_extract(sl, half, VIDX[m])
            if nb < 4:
                zi2 = 2 + nb % 2
                h_.mms([(Z[zi2], ccb[:, k, :], wmb[sl][:, k, :], k == 0, k == 7) for k in range(8)],
                       r=["ccb"] + ["wmb%d_%d" % (sl, k) for k in range(8)], w=[ZK[zi2]])
                h_.tt("dve", modbc[sl], Z[zi2], bb[sl], ALU.add, r=[ZK[zi2], "bb%d" % sl], w=["modbc%d" % sl])
                diag_extract(sl, half, 2 + m)
        for vi in (1, 3, 5):
            h_.ts("dve", vecs[:, vi, :], vecs[:, vi, :], 1.0, None, ALU.add, r=["vecs"], w=["vecs"])

        for kc in range(2):
            h_.dma("sp", sstg[:, 0:768], w_uq[kc * 128:(kc + 1) * 128, :], w=["sstg"])
            h_.ts("dve", w_uqb[:, kc, :], sstg[:, 0:768], qg[:, kc:kc + 1], None, ALU.mult, r=["sstg", "qg"], w=["w_uqb"])
            h_.dma("sp", sstg2[:, 0:256], w_uq_sw[kc * 128:(kc + 1) * 128, :], w=["sstg2"])
            h_.memset("pool", w_uqsb[:, kc, :], 0.0, w=["w_uqsb"])
            h_.ts("dve", w_uqsb[:, kc, :].rearrange("p (h d) -> p h d", d=96)[:, :, 64:96],
                  sstg2[:, 0:256].rearrange("p (h d) -> p h d", d=32), qg[:, kc:kc + 1], None, ALU.mult,
                  r=["sstg2", "qg", "w_uqsb"], w=["w_uqsb"])
        h_.dma("sp", sstg[:, 0:1024], w_ukv, w=["sstg"])
        ukv_v = sstg[:, 0:1024].rearrange("p (h t d) -> p h t d", t=2, d=64)
        h_.ts("dve", w_kb.rearrange("p (h d) -> p h d", d=64), ukv_v[:, :, 0, :], kvg[:, 0:1], None, ALU.mult, r=["sstg", "kvg"], w=["w_kb"])
        h_.ts("dve", w_vb.rearrange("p (h d) -> p h d", d=64), ukv_v[:, :, 1, :], kvg[:, 0:1], None, ALU.mult, r=["sstg", "kvg"], w=["w_vb"])

        P.barrier()

        R1.reset(); R2.reset()
        cqnT = R2.alloc([2, S], BF16)
        ckvnT = R2.alloc([NKEY], BF16)
        KT = [R2.alloc([NKEY], BF16) for _ in range(2)]
        r2_ab = R2.off
        w_inb = R1.alloc([8, INC], BF16)
        xmT = [R1.alloc([8, 512], BF16) for _ in range(2)]
        U = [R1.alloc([4, 514], F32) for _ in range(2)]
        GB = [R1.alloc([4, 512], F32) for _ in range(2)]
        WB = R2.alloc([8, 96], BF16)
        WC = R2.alloc([8, 96], BF16)
        xt = [R2.alloc([1024], F32) for _ in range(2)]
        xn = [R2.alloc([1024], BF16) for _ in range(2)]
        junk = R2.alloc([1024], BF16)
        sq = [R2.alloc([512], BF16) for _ in range(2)]
        csbA = R2.alloc([8, 128], BF16)
        wmbA = R2.alloc([8, 512], BF16)
        bbA = R2.alloc([512], F32)
        tb = [R2.alloc([512], F32) for _ in range(2)]
        rb = [R2.alloc([512], F32) for _ in range(2)]
        gcs = [R2.alloc([512], F32) for _ in range(2)]
        ycv = [R2.alloc([512], F32) for _ in range(2)]
        convT = [R2.alloc([512], BF16) for _ in range(2)]
        cosb = [R2.alloc([512], F32) for _ in range(2)]
        sinb = [R2.alloc([512], F32) for _ in range(2)]
        rt1 = R2.alloc([512], F32)
        rt2 = R2.alloc([512], F32)
        kst = R2.alloc([8, 32], F32)

        for k in range(8):
            h_.dma("pool", w_inb[:, k, :], w_in[k * 128:(k + 1) * 128, :], w=["w_inb"])
        h_.memset("pool", WB, 0.0, w=["WB"])
        h_.memset("pool", WC, 0.0, w=["WC"])
        h_.copy("pool", WB[:, :, 64:96], w_inb[:, :, 384:416], r=["w_inb", "WB"], w=["WB"])
        h_.dma("sp", kst, w_kr_sw.rearrange("(k p) c -> p k c", p=128), w=["kst"])
        h_.copy("dve", WC[:, :, 64:96], kst, r=["kst", "WC"], w=["WC"])
        for b_ in range(2):
            for c in range(4):
                h_.memset("pool", U[b_][:, c, :], 0.0, w=["U%d_%d" % (b_, c)])
        for j in range(8):
            h_.copy("dve", csbA[:, j, :], css[:, j:j + 1].to_broadcast([128, 128]), r=["css"], w=["csbA"])

        def deferred_dma(t):
            nb = 4 + t
            for k in range(0, 8, 2):
                h_.dma("pool", wmbA[:, k:k + 2, :], w_mod_v[:, k:k + 2, nb * 512:(nb + 1) * 512], w=["wmbA"])
            h_.dma("sp", bbA, b_mod[nb * 512:(nb + 1) * 512].partition_broadcast(128), w=["bbA"])

        def deferred_mod(t):
            nb = 4 + t
            m, half = nb // 2, nb % 2
            zi = znext()
            h_.mms([(Z[zi], csbA[:, k, :], wmbA[:, k, :], k == 0, k == 7) for k in range(8)], r=["csbA", "wmbA"], w=[ZK[zi]])
            if m in (2, 5):
                gb_ = g1_bc if m == 2 else g2_bc
                h_.tt("dve", gb_[:, half * 512:(half + 1) * 512], Z[zi], bbA, ALU.add, r=[ZK[zi], "bbA"], w=["gbc%d_%d" % (m, half)])
            else:
                h_.tt("dve", rt1, Z[zi], bbA, ALU.add, r=[ZK[zi], "bbA"], w=["rt1"])
                for jj in range(4):
                    j = half * 4 + jj
                    h_.tt("dve", rt2[:, 0:128], rt1[:, jj * 128:(jj + 1) * 128], identf, ALU.mult, r=["rt1", "identf"], w=["rt2"])
                    h_.rsum(vecs[:, VIDX[m], j:j + 1], rt2[:, 0:128], r=["rt2"], w=["vecs2"])
            if nb == 11:
                pass

        st_ = dict(g=0, z=0, m=0, q=0, w1=0, stat=0)

        def znext():
            zi = st_["z"] % 4
            st_["z"] += 1
            return zi

        def nt_a(row):
            g = st_["g"]
            st_["g"] += 1
            s2 = g % 2
            gc_ = g % 40
            h_.dma("sp", xt[s2], row, w=["xt%d" % s2])
            h_.act(junk, xt[s2], AF.Square, r=["xt%d" % s2], w=["junk", "ssA%d" % gc_], accum=ssA[:, gc_:gc_ + 1])
            h_.act(ssB[:, gc_:gc_ + 1], ssA[:, gc_:gc_ + 1], AF.Ln, r=["ssA%d" % gc_, "epsb"], w=["ssB%d" % gc_], scale=1.0 / D, bias=epsb[:, 0:1])
            h_.act(rsA[:, gc_:gc_ + 1], ssB[:, gc_:gc_ + 1], AF.Exp, r=["ssB%d" % gc_], w=["rsA%d" % gc_], scale=-0.5)
            h_.act(xn[s2], xt[s2], AF.Copy, r=["xt%d" % s2, "rsA%d" % gc_], w=["xn%d" % s2], scale=rsA[:, gc_:gc_ + 1])
            return s2

        def nt_b(s2, i, xm, sci, shi, xmkey):
            tpb = Mps[s2].bitcast(BF16)
            h_.trs([(tpb[:, j * 128:(j + 1) * 128], xn[s2][:, j * 128:(j + 1) * 128]) for j in range(8)], idb, r=["xn%d" % s2, "idb"], w=["M%d" % s2])
            for j in range(8):
                if j % 2 == 0:
                    h_.ts("dve", xm[:, j, i * 128:(i + 1) * 128], tpb[:, j * 128:(j + 1) * 128], vecs[:, sci, j:j + 1], vecs[:, shi, j:j + 1],
                          ALU.mult, ALU.add, r=["M%d" % s2, "vecs"], w=[xmkey])
                else:
                    h_.act(xm[:, j, i * 128:(i + 1) * 128], tpb[:, j * 128:(j + 1) * 128], AF.Identity, r=["M%d" % s2, "vecs"], w=[xmkey],
                           scale=vecs[:, sci, j:j + 1], bias=vecs[:, shi, j:j + 1])

        def zmm(zi, wcol_lo, width, xm, ncol, xmkey, wt=None):
            lst = []
            for k in range(8):
                lhs = w_inb[:, k, wcol_lo:wcol_lo + width] if wt is None else wt[:, k, 0:width]
                lst.append((Z[zi][0:width, 0:ncol], lhs, xm[:, k, 0:ncol], k == 0, k == 7))
            h_.mms(lst, r=[xmkey, "w_inb", "WB", "WC"], w=[ZK[zi]])

        def rms_bc(zis, nfeat, ncol, outs, okeys, osum, okey, s2):
            n = len(zis)
            for i, zi in enumerate(zis):
                h_.act(sq[i][:, 0:ncol], Z[zi][:, 0:ncol], AF.Square, r=[ZK[zi]], w=["sq%d" % i])
            h_.mms([(osum[:, 0:ncol], ones_bf, sq[i][:, 0:ncol], i == 0, i == n - 1) for i in range(n)],
                   r=["ones_bf"] + ["sq%d" % i for i in range(n)], w=[okey])
            h_.act(tb[s2][:, 0:ncol], osum[:, 0:ncol], AF.Ln, r=[okey, "epsb"], w=["tb%d" % s2], scale=1.0 / nfeat, bias=epsb[:, 0:1])
            h_.act(rb[s2][:, 0:ncol], tb[s2][:, 0:ncol], AF.Exp, r=["tb%d" % s2], w=["rb%d" % s2], scale=-0.5)
            for i, zi in enumerate(zis):
                h_.tt("dve", outs[i], Z[zi][:, 0:ncol], rb[s2][:, 0:ncol], ALU.mult, r=[ZK[zi], "rb%d" % s2], w=[okeys[i]])

        def conv_finalize(t):
            s2 = t % 2
            for c in range(4):
                y2 = c % 2
                uk = "U%d_%d" % (s2, c)
                yk = "ycv%d" % y2
                h_.ts("dve", ycv[y2], U[s2][:, c, 1:513], convw[:, c * 3 + 1:c * 3 + 2], None, ALU.mult, r=[uk, "convw"], w=[yk])
                h_.stt(ycv[y2], U[s2][:, c, 0:512], convw[:, c * 3:c * 3 + 1], ycv[y2], ALU.mult, ALU.add, r=[uk, "convw", yk], w=[yk])
                h_.stt(ycv[y2], U[s2][:, c, 2:514], convw[:, c * 3 + 2:c * 3 + 3], ycv[y2], ALU.mult, ALU.add, r=[uk, "convw", yk], w=[yk])
                h_.tt("pool", convT[y2], GB[s2][:, c, :], ycv[y2], ALU.mult, r=["GB%d_%d" % (s2, c), yk], w=["convT%d" % y2])
                h_.dma("pool", acT[512 + c * 128:512 + (c + 1) * 128, t * 512:(t + 1) * 512], convT[y2], r=["convT%d" % y2], w=["acT_c%d" % t])

        def blk_rows(t):
            if t == 8:
                return [ctx[i * 128:(i + 1) * 128, :] for i in range(2)]
            return [x[(t * 4 + i) * 128:(t * 4 + i + 1) * 128, :] for i in range(4)]

        def nt_units(t):
            xb = t % 2
            sci, shi = (3, 2) if t == 8 else (1, 0)
            out = []
            slot = {}
            for i, row in enumerate(blk_rows(t)):
                def fa(i=i, row=row):
                    slot[i] = nt_a(row)

                def fb(i=i):
                    nt_b(slot[i], i, xmT[xb], sci, shi, "xmT%d" % xb)
                out.append((fa, fb))
            return out

        def z_units(t):
            isctx = (t == 8)
            ncol = 256 if isctx else 512
            xb = t % 2
            xmkey = "xmT%d" % xb
            k0 = t * 512
            units = []

            def u_ckv():
                if not isctx:
                    h_.dma("sp", cosb[xb][64:96, :], cos_t[:, k0:k0 + 512], w=["cosb%d" % xb])
                    h_.dma("sp", sinb[xb][64:96, :], sin_t[:, k0:k0 + 512], w=["sinb%d" % xb])
                zi = znext()
                zmm(zi, 256, 128, xmT[xb], ncol, xmkey)
                rms_bc([zi], 128, ncol, [ckvnT[:, k0:k0 + ncol]], ["ckvnT%d" % t], Ops[0], "O0", 0)
            units.append(u_ckv)

            def u_kr():
                ziB = znext()
                zmm(ziB, 0, 96, xmT[xb], ncol, xmkey, wt=WB)
                if isctx:
                    h_.copy("dve", KT[0][64:96, k0:k0 + ncol], Z[ziB][64:96, 0:ncol], r=[ZK[ziB]], w=["KTr0_%d" % t])
                else:
                    ziC = znext()
                    zmm(ziC, 0, 96, xmT[xb], ncol, xmkey, wt=WC)
                    h_.tt("dve", rt1[64:96, :], Z[ziB][64:96, :], cosb[xb][64:96, :], ALU.mult, r=[ZK[ziB], "cosb%d" % xb], w=["rt1"])
                    h_.tt("dve", rt2[64:96, :], Z[ziC][64:96, :], sinb[xb][64:96, :], ALU.mult, r=[ZK[ziC], "sinb%d" % xb], w=["rt2"])
                    h_.tt("dve", KT[0][64:96, k0:k0 + 512], rt1[64:96, :], rt2[64:96, :], ALU.add, r=["rt1", "rt2"], w=["KTr0_%d" % t])
                h_.copy("pool", KT[1][64:96, k0:k0 + ncol], KT[0][64:96, k0:k0 + ncol], r=["KTr0_%d" % t], w=["KTr1_%d" % t])
            units.append(u_kr)
            if isctx:
                return units

            def u_q():
                z0 = znext()
                zmm(z0, 0, 128, xmT[xb], ncol, xmkey)
                z1 = znext()
                zmm(z1, 128, 128, xmT[xb], ncol, xmkey)
                rms_bc([z0, z1], 256, ncol, [cqnT[:, 0, k0:k0 + 512], cqnT[:, 1, k0:k0 + 512]], ["cqnT%d" % t, "cqnT%d" % t], Ops[1], "O1", 1)
            units.append(u_q)
            pb = 1 - xb

            def u_conv(c):
                g2 = c % 2
                uk = "U%d_%d" % (xb, c)
                pk = "U%d_%d" % (pb, c)
                zb = znext()
                zmm(zb, 416 + c * 128, 128, xmT[xb], ncol, xmkey)
                h_.copy("act", GB[xb][:, c, :], Z[zb], r=[ZK[zb]], w=["GB%d_%d" % (xb, c)])
                zc = znext()
                zmm(zc, 928 + c * 128, 128, xmT[xb], ncol, xmkey)
                h_.copy("act", gcs[g2], Z[zc], r=[ZK[zc]], w=["gcs%d" % g2])
                zx = znext()
                zmm(zx, 1440 + c * 128, 128, xmT[xb], ncol, xmkey)
                h_.tt("dve", U[xb][:, c, 1:513], Z[zx], gcs[g2], ALU.mult, r=[ZK[zx], "gcs%d" % g2], w=[uk])
                if t > 0:
                    h_.copy("pool", U[xb][:, c, 0:1], U[pb][:, c, 512:513], r=[pk, uk], w=[uk])
                    h_.copy("pool", U[pb][:, c, 513:514], U[xb][:, c, 1:2], r=[uk, pk], w=[pk])
                else:
                    h_.memset("pool", U[xb][:, c, 0:1], 0.0, r=[uk], w=[uk])
            for c in range(4):
                units.append(lambda c=c: u_conv(c))

            def u_fin():
                if t > 0:
                    conv_finalize(t - 1)
                if t == 7:
                    for c in range(4):
                        h_.memset("pool", U[xb][:, c, 513:514], 0.0, r=["U%d_%d" % (xb, c)], w=["U%d_%d" % (xb, c)])
                    conv_finalize(7)
            units.append(u_fin)
            return units

        for fa, fb in nt_units(0):
            fa()
            fb()
        for t in range(9):
            nxt = nt_units(t + 1) if t + 1 < 9 else []
            na = 0
            nb_ = 0
            for idx, u in enumerate(z_units(t)):
                u()
                if idx >= 2 and nb_ < na:
                    nxt[nb_][1]()
                    nb_ += 1
                if na < len(nxt):
                    nxt[na][0]()
                    na += 1
            while nb_ < len(nxt):
                if na <= nb_:
                    nxt[na][0]()
                    na += 1
                nxt[nb_][1]()
                nb_ += 1
                if na < len(nxt):
                    nxt[na][0]()
                    na += 1

        P.barrier()

        R1.reset(); R2.reset(r2_ab)
        w_outb = R1.alloc([8, 1024], BF16)
        w2b = R1.alloc([32, 1024], BF16)
        V = [R2.alloc([NKT, 65], BF16) for _ in range(2)]
        qT = [R2.alloc([S], BF16) for _ in range(2)]
        PT = [R2.alloc([1024], BF16) for _ in range(3)]
        qcos = [R2.alloc([512], F32) for _ in range(2)]
        qsin = [R2.alloc([512], F32) for _ in range(2)]
        qt1 = R2.alloc([512], F32)
        qt2 = R2.alloc([512], F32)
        den = R2.alloc([512], F32)
        rec = R2.alloc([512], F32)
        aoT = [R2.alloc([512], BF16) for _ in range(2)]
        bstg = [R2.alloc([1024], F32) for _ in range(2)]

        for b_ in range(2):
            h_.memset("pool", V[b_][:, :, 64:65], 1.0, w=["Vones%d" % b_])

        wq = [("o", j) for j in range(8)] + [("2", j) for j in range(32)]

        w1tmp = w2b[:, 28:32, :]
        w1s_w = w1s.rearrange("f p (k c) -> p f k c", k=8)

        def precast_w1(k):
            h_.dma("pool", w1tmp, w1[k * 128:(k + 1) * 128, :].rearrange("p (a b) -> p a b", b=1024), w=["w1tmp"])
            h_.dma("sp", w1s_w[:, :, k, :], w1tmp.rearrange("p a (g c) -> p (a g) c", c=128), r=["w1tmp"], w=["w1s"])

        def load_some_weights(n):
            for _ in range(n):
                if st_["q"] >= len(wq):
                    return
                kind, j = wq[st_["q"]]
                s2 = st_["q"] % 2
                st_["q"] += 1
                if kind == "o":
                    h_.dma("sp", bstg[s2], w_out[j * 128:(j + 1) * 128, :], w=["bstg%d" % s2])
                    h_.tt("pool", w_outb[:, j, :], bstg[s2], g1_bc, ALU.mult, r=["bstg%d" % s2], w=["w_outb"])
                else:
                    h_.dma("sp", bstg[s2], w2[j * 128:(j + 1) * 128, :], w=["bstg%d" % s2])
                    h_.tt("pool", w2b[:, j, :], bstg[s2], g2_bc, ALU.mult, r=["bstg%d" % s2], w=["w2b"] + (["w1tmp"] if j >= 28 else []))

        def mnext():
            m = st_["m"] % 2
            st_["m"] += 1
            return m

        NP = NKT // 2
        qdc = [0]

        def setup_units(h):
            hb = h % 2
            units = []

            def kunit(kb):
                n = 512 if kb < 8 else 256
                m = mnext()
                h_.mms([(Mps[m][0:64, 0:n], w_kb[:, h * 64:(h + 1) * 64], ckvnT[:, kb * 512:kb * 512 + n], True, True)], r=["w_kb", "ckvnT%d" % kb], w=["M%d" % m])
                h_.copy("dve", KT[hb][0:64, kb * 512:kb * 512 + n], Mps[m][0:64, 0:n], r=["M%d" % m], w=["KTn%d_%d" % (hb, kb)])

            def vunit(g0):
                ng = min(8, NKT - g0)
                m = mnext()
                h_.mms([(Mps[m][:, jj * 64:(jj + 1) * 64], ckvnT[:, (g0 + jj) * 128:(g0 + jj + 1) * 128], w_vb[:, h * 64:(h + 1) * 64], True, True) for jj in range(ng)],
                       r=["w_vb"] + ["ckvnT%d" % kb for kb in range(9)], w=["M%d" % m])
                h_.copy("dve", V[hb][:, g0:g0 + ng, 0:64], Mps[m][:, 0:ng * 64].rearrange("p (a b) -> p a b", b=64), r=["M%d" % m], w=["V%d" % hb])

            def qunit(qb):
                q0 = qb * 512
                cs2 = qdc[0] % 2
                qdc[0] += 1
                h_.dma("sp", qcos[cs2][64:96, :], cos_t[:, q0:q0 + 512], w=["qcos%d" % cs2])
                h_.dma("sp", qsin[cs2][64:96, :], sin_t[:, q0:q0 + 512], w=["qsin%d" % cs2])
                ma = mnext()
                mb = mnext()
                h_.mms([(Mps[ma][0:96, :], w_uqb[:, kc, h * 96:(h + 1) * 96], cqnT[:, kc, q0:q0 + 512], kc == 0, kc == 1) for kc in range(2)], r=["w_uqb", "cqnT%d" % qb], w=["M%d" % ma])
                h_.mms([(Mps[mb][0:96, :], w_uqsb[:, kc, h * 96:(h + 1) * 96], cqnT[:, kc, q0:q0 + 512], kc == 0, kc == 1) for kc in range(2)], r=["w_uqsb", "cqnT%d" % qb], w=["M%d" % mb])
                qk_ = "qT%d_%d" % (hb, qb)
                h_.copy("dve", qT[hb][0:64, q0:q0 + 512], Mps[ma][0:64, :], r=["M%d" % ma], w=[qk_])
                h_.tt("dve", qt1[64:96, :], Mps[ma][64:96, :], qcos[cs2][64:96, :], ALU.mult, r=["M%d" % ma, "qcos%d" % cs2], w=["qt1"])
                h_.tt("dve", qt2[64:96, :], Mps[mb][64:96, :], qsin[cs2][64:96, :], ALU.mult, r=["M%d" % mb, "qsin%d" % cs2], w=["qt2"])
                h_.tt("dve", qT[hb][64:96, q0:q0 + 512], qt1[64:96, :], qt2[64:96, :], ALU.add, r=["qt1", "qt2"], w=[qk_])

            for kb in range(9):
                units.append(lambda kb=kb: kunit(kb))
            for g0 in range(0, NKT, 8):
                units.append(lambda g0=g0: vunit(g0))
            for qb in range(8):
                units.append(lambda qb=qb: qunit(qb))
            return units

        osb = R2.alloc([512], F32)
        steps = [(h, qb, jp) for h in range(NH) for qb in range(8) for jp in range(NP)]
        NS_ = len(steps)

        def qk(s):
            h, qb, jp = steps[s]
            hb = h % 2
            q0 = qb * 512
            sp_ = s % 2
            lst = []
            keys = ["qT%d_%d" % (hb, qb)]
            for u in range(2):
                j = 2 * jp + u
                kb = (j * 128) // 512
                keys += ["KTn%d_%d" % (hb, kb), "KTr%d_%d" % (hb, kb)]
                lst.append((Sps[sp_][:, u * 512:(u + 1) * 512], KT[hb][0:96, j * 128:(j + 1) * 128], qT[hb][0:96, q0:q0 + 512], True, True))
            h_.mms(lst, r=keys, w=["S%d" % sp_])

        def ex(s):
            sp_ = s % 2
            pp = s % 3
            h_.act(PT[pp], Sps[sp_], AF.Exp, r=["S%d" % sp_], w=["PT%d" % pp], scale=ATTN_SCALE)

        def pv(s):
            h, qb, jp = steps[s]
            hb = h % 2
            ob = qb % 2
            pp = s % 3
            lst = []
            for u in range(2):
                j = 2 * jp + u
                lst.append((Ops[ob][0:65, :], V[hb][:, j, 0:65], PT[pp][:, u * 512:(u + 1) * 512], j == 0, j == NKT - 1))
            h_.mms(lst, r=["V%d" % hb, "Vones%d" % hb, "PT%d" % pp], w=["O%d" % ob])

        def norm1(h, qb):
            ob = qb % 2
            h_.copy("dve", den[64:65, :], Ops[ob][64:65, :], r=["O%d" % ob], w=["den"])
            h_.copy("dve", osb[0:64, :], Ops[ob][0:64, :], r=["O%d" % ob], w=["osb"])

        def norm2(h, qb):
            m = mnext()
            h_.mms([(Mps[m][0:64, :], onesf[64:65, 0:64], den[64:65, :], True, True)], r=["den", "onesf"], w=["M%d" % m])
            h_.recip(rec[0:64, :], Mps[m][0:64, :], r=["M%d" % m], w=["rec"])
            a2 = (h * 8 + qb) % 2
            h_.tt("dve", aoT[a2][0:64, :], osb[0:64, :], rec[0:64, :], ALU.mult, r=["osb", "rec"], w=["aoT%d" % a2])
            h_.dma("pool", acT[h * 64:(h + 1) * 64, qb * 512:(qb + 1) * 512], aoT[a2][0:64, :], r=["aoT%d" % a2], w=["acT_a%d_%d" % (h, qb)])

        for u in setup_units(0):
            u()
        qk(0)
        ex(0)
        qk(1)
        ex(1)
        pending = None
        nxt = []
        cnt = 0
        for s in range(NS_):
            h, qb, jp = steps[s]
            if jp == 0 and qb == 0:
                if h < 4:
                    precast_w1(2 * h)
                    precast_w1(2 * h + 1)
                load_some_weights(5)
                nxt = setup_units(h + 1) if h + 1 < NH else []
                cnt = 0
            if s + 2 < NS_:
                if steps[s + 2][0] != h:
                    while nxt:
                        nxt.pop(0)()
                qk(s + 2)
            pv(s)
            if s + 2 < NS_:
                ex(s + 2)
            if pending is not None and pending[0] == s:
                norm2(pending[1], pending[2])
                pending = None
            if jp == NP - 1:
                norm1(h, qb)
                pending = (s + 2, h, qb)
            cnt += 1
            if nxt and cnt % 5 == 3:
                nxt.pop(0)()
        if pending is not None:
            norm2(pending[1], pending[2])
        load_some_weights(100)

        P.barrier()

        R2.reset()
        xs = [[R2.alloc([1024], F32) for _ in range(4)] for _ in range(2)]
        acb = R2.alloc([8, 512], BF16)
        xn2 = [R2.alloc([1024], BF16) for _ in range(4)]
        xm2T = R2.alloc([8, 512], BF16)
        hT = R2.alloc([32, 512], BF16)
        hr = [R2.alloc([512], F32) for _ in range(2)]
        junk2 = hr[0].bitcast(BF16)
        w1c = [R2.alloc([8, 128], BF16) for _ in range(5)]
        w1_v = w1.rearrange("(k p) n -> p k n", p=128)

        def stats(src, srckey):
            g = st_["stat"] % 40
            st_["stat"] += 1
            h_.act(junk2, src, AF.Square, r=[srckey], w=["hr0", "ssA%d" % g], accum=ssA[:, g:g + 1])
            h_.act(ssB[:, g:g + 1], ssA[:, g:g + 1], AF.Ln, r=["ssA%d" % g, "epsb"], w=["ssB%d" % g], scale=1.0 / D, bias=epsb[:, 0:1])
            h_.act(rsA[:, g:g + 1], ssB[:, g:g + 1], AF.Exp, r=["ssB%d" % g], w=["rsA%d" % g], scale=-0.5)
            return g

        def load_block(t):
            for i in range(4):
                h_.dma("sp", xs[t % 2][i], x[t * 512 + i * 128:t * 512 + (i + 1) * 128, :], w=["x%d_%d" % (t % 2, i)])

        def load_acb(t):
            for j in range(8):
                h_.dma("sp", acb[:, j, :], acT[j * 128:(j + 1) * 128, t * 512:(t + 1) * 512], w=["acb"])

        def outproj_stats(t):
            X = xs[t % 2]
            for i in range(4):
                xk = "x%d_%d" % (t % 2, i)
                for hf in range(2):
                    zi = znext()
                    h_.mms([(Z[zi], acb[:, j, i * 128:(i + 1) * 128], w_outb[:, j, hf * 512:(hf + 1) * 512], j == 0, j == 7) for j in range(8)], r=["acb", "w_outb"], w=[ZK[zi]])
                    h_.tt("dve", X[i][:, hf * 512:(hf + 1) * 512], Z[zi], X[i][:, hf * 512:(hf + 1) * 512], ALU.add, r=[ZK[zi], xk], w=[xk])
                g = stats(X[i], xk)
                h_.act(xn2[i], X[i], AF.Copy, r=[xk, "rsA%d" % g], w=["xn2_%d" % i], scale=rsA[:, g:g + 1])

        def trans(i):
            s2 = i % 2
            tpb = Mps[s2].bitcast(BF16)
            h_.trs([(tpb[:, j * 128:(j + 1) * 128], xn2[i][:, j * 128:(j + 1) * 128]) for j in range(8)], idb, r=["xn2_%d" % i, "idb"], w=["M%d" % s2])
            for j in range(8):
                if j % 2 == 0:
                    h_.ts("dve", xm2T[:, j, i * 128:(i + 1) * 128], tpb[:, j * 128:(j + 1) * 128], vecs[:, 5, j:j + 1], vecs[:, 4, j:j + 1],
                          ALU.mult, ALU.add, r=["M%d" % s2, "vecs"], w=["xm2T"])
                else:
                    h_.act(xm2T[:, j, i * 128:(i + 1) * 128], tpb[:, j * 128:(j + 1) * 128], AF.Identity, r=["M%d" % s2, "vecs"], w=["xm2T"],
                           scale=vecs[:, 5, j:j + 1], bias=vecs[:, 4, j:j + 1])

        def mlp1(t):
            for f_ in range(32):
                n = st_["w1"]
                st_["w1"] += 1
                c4 = n % 5
                h_.dma("sp", w1c[c4], w1s[f_].rearrange("p (k c) -> p k c", k=8), w=["w1c%d" % c4])
                zi = znext()
                h_.mms([(Z[zi], w1c[c4][:, k, :], xm2T[:, k, :], k == 0, k == 7) for k in range(8)], r=["w1c%d" % c4, "xm2T"], w=[ZK[zi]])
                h2 = f_ % 2
                h_.act(hr[h2], Z[zi], AF.Relu, r=[ZK[zi]], w=["hr%d" % h2])
                h_.tt("dve", hT[:, f_, :], hr[h2], hr[h2], ALU.mult, r=["hr%d" % h2], w=["hT%d" % f_])

        def mlp2_final(t, with_trans):
            X = xs[t % 2]
            for i in range(4):
                xk = "x%d_%d" % (t % 2, i)
                for hf in range(2):
                    zi = znext()
                    h_.mms([(Z[zi], hT[:, k, i * 128:(i + 1) * 128], w2b[:, k, hf * 512:(hf + 1) * 512], k == 0, k == 31) for k in range(32)],
                           r=["w2b"] + ["hT%d" % k for k in range(32)], w=[ZK[zi]])
                    h_.tt("dve", X[i][:, hf * 512:(hf + 1) * 512], Z[zi], X[i][:, hf * 512:(hf + 1) * 512], ALU.add, r=[ZK[zi], xk], w=[xk])
                if with_trans:
                    trans(i)
                g = stats(X[i], xk)
                h_.stt(X[i], X[i], rsA[:, g:g + 1], gf_bc, ALU.mult, ALU.mult, r=[xk, "rsA%d" % g, "gf_bc"], w=[xk])
                h_.dma("pool", y[t * 512 + i * 128:t * 512 + (i + 1) * 128, :], X[i], r=[xk], w=["y%d_%d" % (t, i)])

        load_block(0)
        load_acb(0)
        load_block(1)
        outproj_stats(0)
        load_acb(1)
        for i in range(4):
            trans(i)
        for t in range(8):
            mlp1(t)
            if t + 1 < 8:
                outproj_stats(t + 1)
                if t + 2 < 8:
                    load_acb(t + 2)
            mlp2_final(t, t + 1 < 8)
            if t + 2 < 8:
                load_block(t + 2)

        P.emit(nc, block, st)
    return nc


def _host_inputs(x, c, ctx, c_ctx, w_mod, b_mod, w_in, q_norm_g, w_uq, kv_norm_g, w_ukv,
                 conv_w, w_out, w_mlp1, w_mlp2, final_norm_g):
    f = lambda a: np.ascontiguousarray(np.asarray(a), dtype=np.float32)
    x, c, ctx, c_ctx = f(x), f(c), f(ctx), f(c_ctx)
    w_mod, b_mod, w_in = f(w_mod)[0], f(b_mod)[0], f(w_in)[0]
    q_norm_g, w_uq, kv_norm_g, w_ukv = f(q_norm_g)[0], f(w_uq)[0], f(kv_norm_g)[0], f(w_ukv)[0]
    conv_w, w_out, w_mlp1, w_mlp2, gfin = f(conv_w)[0], f(w_out)[0], f(w_mlp1)[0], f(w_mlp2)[0], f(final_norm_g)
    perm = np.array(list(range(8, 16)) + list(range(0, 8)) + list(range(24, 32)) + list(range(16, 24)))
    w_kr_sw = f(w_in[:, 384:416][:, perm])
    w_uq_sw = f(w_uq.reshape(256, 8, 96)[:, :, 64:96][:, :, perm].reshape(256, 256))
    pos = np.arange(S)
    row = (pos // 64).astype(np.float32)
    col = (pos % 64).astype(np.float32)
    freqs = (np.float32(10000.0) ** (-np.arange(0, 16, 2, dtype=np.float32) / np.float32(16))).astype(np.float32)
    ar = (row[None, :] * freqs[:, None]).astype(np.float32)
    ac = (col[None, :] * freqs[:, None]).astype(np.float32)
    cos_t = np.concatenate([np.cos(ar), np.cos(ar), np.cos(ac), np.cos(ac)], 0).astype(np.float32)
    sin_t = np.concatenate([-np.sin(ar), np.sin(ar), -np.sin(ac), np.sin(ac)], 0).astype(np.float32)
    shared = dict(
        cc_fm=f(c_ctx.reshape(8, 128).T), w_mod=w_mod, b_mod=b_mod, w_in=w_in, w_kr_sw=w_kr_sw,
        qg_fm=f(q_norm_g.reshape(2, 128).T), w_uq=w_uq, w_uq_sw=w_uq_sw, kvg_fm=f(kv_norm_g.reshape(1, 128).T),
        w_ukv=w_ukv, convw_fm=f(conv_w.reshape(3, 4, 128).transpose(2, 1, 0).reshape(128, 12)),
        w_out=w_out, w_mlp1=w_mlp1, w_mlp2=w_mlp2, gf=gfin, ident=np.eye(128, dtype=np.float32),
        cos_t=cos_t, sin_t=sin_t)
    maps = []
    for b in range(8):
        m = dict(shared)
        m["x"] = x[b]
        m["ctx"] = ctx[b]
        m["c_fm"] = f(c[b].reshape(8, 128).T)
        maps.append(m)
    return maps


_NC = None


def kernel(**inputs):
    global _NC
    maps = _host_inputs(**inputs)
    if _NC is None:
        _NC = build()
    res = run_bass_kernel_spmd(_NC, maps, core_ids=list(range(8)))
    return np.stack([np.asarray(r["y"]) for r in res.results], 0).astype(np.float32)
```

```python
import math
import numpy as np
from contextlib import ExitStack
import concourse.bass as bass
import concourse.mybir as mybir
from concourse.bass_utils import run_bass_kernel_spmd

F32 = mybir.dt.float32
BF16 = mybir.dt.bfloat16
ALU = mybir.AluOpType
AF = mybir.ActivationFunctionType

D = 1024
S = 4096
L = 256
NH = 8
NKEY = S + L
NKT = NKEY // 128
INC = 1952
EPS = 1e-6
ATTN_SCALE = 1.0 / math.sqrt(96.0)

ENGS = ["pe", "act", "dve", "pool", "sp"]
N_DMA_SEMS = {"sp": 12, "pool": 10}


class Prog:
    def __init__(self):
        self.ops = []
        self.lw = {}
        self.rd = {}
        self.eng_ops = {e: [] for e in ENGS}
        self.ndma = {e: 0 for e in ENGS}
        self.pending = {}
        self.last_op = {}
        self.last_dma = {}

    def op(self, eng, fn, r=(), w=(), dma=False):
        i = len(self.ops)
        deps = set()
        for k in r:
            if k in self.lw:
                deps.add(self.lw[k])
        for k in w:
            if k in self.lw:
                deps.add(self.lw[k])
            deps.update(self.rd.get(k, ()))
        if eng in self.pending:
            deps |= self.pending.pop(eng)
        for k in r:
            self.rd.setdefault(k, []).append(i)
        for k in w:
            self.lw[k] = i
            self.rd[k] = []
        o = dict(eng=eng, fn=fn, deps=deps, dma=dma, sig=False)
        if dma:
            k = self.ndma[eng]
            o["dma_k"] = k
            self.ndma[eng] += 1
            self.last_dma[(eng, k % N_DMA_SEMS[eng])] = i
        else:
            self.last_op[eng] = i
        self.ops.append(o)
        self.eng_ops[eng].append(i)
        return i

    def dma(self, eng, out, in_, r=(), w=()):
        return self.op(eng, lambda e: e.dma_start(out=out, in_=in_), r=r, w=w, dma=True)

    def barrier(self):
        deps = set(self.last_op.values()) | set(self.last_dma.values())
        for e in ENGS:
            self.pending[e] = set(deps) | self.pending.get(e, set())

    def emit(self, nc, block, semstack):
        ops = self.ops
        engsem = {e: semstack.enter_context(nc.semaphore("s_" + e)) for e in ENGS}
        dmasem = {
            e: [semstack.enter_context(nc.semaphore("d_%s%d" % (e, i))) for i in range(n)]
            for e, n in N_DMA_SEMS.items()
        }
        for o in ops:
            for d in o["deps"]:
                dd = ops[d]
                if (not dd["dma"]) and dd["eng"] == o["eng"] and o["eng"] == "pe":
                    continue
                dd["sig"] = True
        cnt = {e: 0 for e in ENGS}
        for o in ops:
            if o["dma"]:
                n = N_DMA_SEMS[o["eng"]]
                o["sem"] = dmasem[o["eng"]][o["dma_k"] % n]
                o["val"] = 16 * (o["dma_k"] // n + 1)
            elif o["sig"]:
                cnt[o["eng"]] += 1
                o["sem"] = engsem[o["eng"]]
                o["val"] = cnt[o["eng"]]

        def run(engname, e, final=False):
            known = {}
            for i in self.eng_ops[engname]:
                o = ops[i]
                waits = {}
                for d in o["deps"]:
                    dd = ops[d]
                    if not dd["sig"] and not dd["dma"]:
                        continue
                    s, v = dd["sem"], dd["val"]
                    key = id(s)
                    if key not in waits or waits[key][1] < v:
                        waits[key] = (s, v)
                if o["dma"] and o["val"] > 16:
                    s = o["sem"]
                    key = id(s)
                    v = o["val"] - 16
                    if key not in waits or waits[key][1] < v:
                        waits[key] = (s, v)
                for key, (s, v) in waits.items():
                    if known.get(key, 0) < v:
                        e.wait_ge(s, v)
                        known[key] = v
                ins = o["fn"](e)
                if o["dma"]:
                    ins.then_inc(o["sem"], 16)
                elif o["sig"]:
                    ins.then_inc(o["sem"], 1)
            if final:
                last = {}
                for o in ops:
                    if o["dma"]:
                        key = id(o["sem"])
                        if key not in last or last[key][1] < o["val"]:
                            last[key] = (o["sem"], o["val"])
                for key, (s, v) in last.items():
                    if known.get(key, 0) < v:
                        e.wait_ge(s, v)

        @block.tensor
        def _(e):
            run("pe", e)

        @block.scalar
        def _(e):
            run("act", e)

        @block.vector
        def _(e):
            run("dve", e)

        @block.gpsimd
        def _(e):
            run("pool", e)

        @block.sync
        def _(e):
            run("sp", e, final=True)


class Arena:
    def __init__(self, t, base, cap):
        self.t = t
        self.base = base
        self.cap = cap
        self.off = 0

    def reset(self, off=0):
        self.off = off

    def alloc(self, shape, dt):
        esz = 4 if dt == F32 else 2
        ne = int(np.prod(shape))
        n = (ne * esz + 31) // 32 * 32
        o = self.base + self.off
        assert self.off + n <= self.cap, ("arena overflow", self.off, n, self.cap)
        self.off += n
        v = self.t[:, o // 4:(o + n) // 4]
        if dt != F32:
            v = v.bitcast(dt)
        v = v[:, 0:ne]
        if len(shape) == 2:
            v = v.rearrange("p (a b) -> p a b", b=shape[1])
        elif len(shape) == 3:
            v = v.rearrange("p (a b c) -> p a b c", b=shape[1], c=shape[2])
        return v


class H:
    def __init__(self, P):
        self.P = P
        self.rr = 0

    def dma(self, eng, out, in_, r=(), w=()):
        return self.P.op(eng, lambda e: e.dma_start(out=out, in_=in_), r=r, w=w, dma=True)

    def copy(self, eng, out, in_, r=(), w=()):
        if eng == "act":
            return self.P.op("act", lambda e: e.activation(out=out, in_=in_, func=AF.Copy), r=r, w=w)
        return self.P.op(eng, lambda e: e.tensor_copy(out, in_), r=r, w=w)

    def cast(self, out, in_, r=(), w=(), engs=("dve", "pool", "act")):
        e = engs[self.rr % len(engs)]
        self.rr += 1
        return self.copy(e, out, in_, r, w)

    def memset(self, eng, ap, val, r=(), w=()):
        return self.P.op(eng, lambda e: e.memset(ap, val), r=r, w=w)

    def tt(self, eng, out, in0, in1, op, r=(), w=()):
        return self.P.op(eng, lambda e: e.tensor_tensor(out=out, in0=in0, in1=in1, op=op), r=r, w=w)

    def ts(self, eng, out, in0, s1, s2, op0, op1=None, r=(), w=()):
        if op1 is None:
            return self.P.op(eng, lambda e: e.tensor_scalar(out=out, in0=in0, scalar1=s1, scalar2=None, op0=op0), r=r, w=w)
        return self.P.op(eng, lambda e: e.tensor_scalar(out=out, in0=in0, scalar1=s1, scalar2=s2, op0=op0, op1=op1), r=r, w=w)

    def stt(self, out, in0, scalar, in1, op0, op1, r=(), w=()):
        return self.P.op("dve", lambda e: e.scalar_tensor_tensor(out=out, in0=in0, scalar=scalar, in1=in1, op0=op0, op1=op1), r=r, w=w)

    def act(self, out, in_, func, r=(), w=(), scale=None, accum=None, bias=None):
        kw = {}
        if bias is not None:
            kw["bias"] = bias
        if scale is not None:
            kw["scale"] = scale
        if accum is not None:
            kw["accum_out"] = accum
        return self.P.op("act", lambda e: e.activation(out=out, in_=in_, func=func, **kw), r=r, w=w)

    def recip(self, out, in_, r=(), w=()):
        return self.P.op("dve", lambda e: e.reciprocal(out=out, in_=in_), r=r, w=w)

    def rsum(self, out, in_, r=(), w=()):
        return self.P.op("dve", lambda e: e.reduce_sum(out=out, in_=in_, axis=mybir.AxisListType.X), r=r, w=w)

    def mms(self, lst, r=(), w=()):
        lst = list(lst)

        def f(e):
            for (o, l, rh, s0, s1) in lst:
                ins = e.matmul(o, lhsT=l, rhs=rh, start=s0, stop=s1)
            return ins
        return self.P.op("pe", f, r=r, w=w)

    def trs(self, lst, ident, r=(), w=()):
        lst = list(lst)

        def f(e):
            for (o, i) in lst:
                ins = e.transpose(out=o, in_=i, identity=ident)
            return ins
        return self.P.op("pe", f, r=r, w=w)


P_BYTES = 23 * 1024
R1_BYTES = 80 * 1024
TOTAL_BYTES = 212800


def build(dbg=False):
    nc = bass.Bass("TRN2", target_bir_lowering=False)

    def din(name, shape, dt=F32):
        return nc.dram_tensor(name, shape, dt, kind="ExternalInput").ap()

    x = din("x", [S, D])
    ctx = din("ctx", [L, D])
    c_fm = din("c_fm", [128, 8])
    cc_fm = din("cc_fm", [128, 8])
    w_mod = din("w_mod", [D, 6 * D])
    b_mod = din("b_mod", [6 * D])
    w_in = din("w_in", [D, INC])
    w_kr_sw = din("w_kr_sw", [D, 32])
    qg_fm = din("qg_fm", [128, 2])
    w_uq = din("w_uq", [256, 768])
    w_uq_sw = din("w_uq_sw", [256, 256])
    kvg_fm = din("kvg_fm", [128, 1])
    w_ukv = din("w_ukv", [128, 1024])
    convw_fm = din("convw_fm", [128, 12])
    w_out = din("w_out", [D, D])
    w1 = din("w_mlp1", [D, 4096])
    w2 = din("w_mlp2", [4096, D])
    gf = din("gf", [D])
    ident = din("ident", [128, 128])
    cos_t = din("cos_t", [32, S])
    sin_t = din("sin_t", [32, S])
    y = nc.dram_tensor("y", [S, D], F32, kind="ExternalOutput").ap()
    acT = nc.dram_tensor("acT", [D, S], BF16, kind="Internal").ap()
    w1s = nc.dram_tensor("w1s", [32, 128, 1024], BF16, kind="Internal").ap()

    P = Prog()
    h_ = H(P)
    with ExitStack() as st:
        arena_t = st.enter_context(nc.sbuf_tensor("arena", [128, TOTAL_BYTES // 4], F32))
        AP_ = Arena(arena_t, 0, P_BYTES)
        R1 = Arena(arena_t, P_BYTES, R1_BYTES)
        R2 = Arena(arena_t, P_BYTES + R1_BYTES, TOTAL_BYTES - P_BYTES - R1_BYTES)
        SA = st.enter_context(nc.psum_tensor("SA", [128, 1024], F32))
        SB = st.enter_context(nc.psum_tensor("SB", [128, 1024], F32))
        O0 = st.enter_context(nc.psum_tensor("O0", [128, 512], F32))
        O1 = st.enter_context(nc.psum_tensor("O1", [128, 512], F32))
        M0 = st.enter_context(nc.psum_tensor("M0", [128, 512], F32))
        M1 = st.enter_context(nc.psum_tensor("M1", [128, 512], F32))
        block = st.enter_context(nc.Block())
        Sps = [SA[:, :], SB[:, :]]
        Ops = [O0[:, :], O1[:, :]]
        Mps = [M0[:, :], M1[:, :]]
        Z = [SA[:, 0:512], SA[:, 512:1024], SB[:, 0:512], SB[:, 512:1024]]
        ZK = ["Z0", "Z1", "Z2", "Z3"]

        idb = AP_.alloc([128], BF16)
        identf = AP_.alloc([128], F32)
        ones_bf = AP_.alloc([128], BF16)
        onesf = AP_.alloc([64], F32)
        mh = AP_.alloc([1], F32)
        epsb = AP_.alloc([1], F32)
        css = AP_.alloc([8], F32)
        vecs = AP_.alloc([8, 8], F32)
        qg = AP_.alloc([2], F32)
        kvg = AP_.alloc([1], F32)
        convw = AP_.alloc([12], F32)
        ssA = AP_.alloc([40], F32)
        ssB = AP_.alloc([40], F32)
        rsA = AP_.alloc([40], F32)
        gf_bc = AP_.alloc([1024], F32)
        g1_bc = AP_.alloc([1024], F32)
        g2_bc = AP_.alloc([1024], F32)
        w_uqb = AP_.alloc([2, 768], BF16)
        w_uqsb = AP_.alloc([2, 768], BF16)
        w_kb = AP_.alloc([512], BF16)
        w_vb = AP_.alloc([512], BF16)

        h_.dma("sp", identf, ident, w=["identf"])
        h_.copy("dve", idb, identf, r=["identf"], w=["idb"])
        h_.memset("pool", ones_bf, 1.0, w=["ones_bf"])
        h_.memset("pool", onesf, 1.0, w=["onesf"])
        h_.memset("pool", mh, -0.5, w=["mh"])
        h_.memset("pool", epsb, EPS, w=["epsb"])
        h_.dma("sp", qg, qg_fm, w=["qg"])
        h_.dma("sp", kvg, kvg_fm, w=["kvg"])
        h_.dma("sp", convw, convw_fm, w=["convw"])
        h_.dma("sp", gf_bc, gf.partition_broadcast(128), w=["gf_bc"])

        R1.reset(); R2.reset()
        cs = R2.alloc([8], F32)
        ccs = R2.alloc([8], F32)
        ccss = R2.alloc([8], F32)
        csb = R2.alloc([8, 128], BF16)
        ccb = R2.alloc([8, 128], BF16)
        bb = [R2.alloc([512], F32) for _ in range(3)]
        modbc = [R2.alloc([512], F32) for _ in range(3)]
        junkf = R2.alloc([128], F32)
        wmb = [R1.alloc([8, 512], BF16) for _ in range(3)]
        sstg = R2.alloc([2048], F32)
        sstg2 = R2.alloc([1024], F32)

        h_.dma("sp", cs, c_fm, w=["cs"])
        h_.dma("sp", ccs, cc_fm, w=["ccs"])
        h_.act(css, cs, AF.Silu, r=["cs"], w=["css"])
        h_.act(ccss, ccs, AF.Silu, r=["ccs"], w=["ccss"])
        for j in range(8):
            h_.copy("dve", csb[:, j, :], css[:, j:j + 1].to_broadcast([128, 128]), r=["css"], w=["csb"])
            h_.copy("dve", ccb[:, j, :], ccss[:, j:j + 1].to_broadcast([128, 128]), r=["ccss"], w=["ccb"])

        w_mod_v = w_mod.rearrange("(k p) n -> p k n", p=128)
        VIDX = {0: 0, 1: 1, 3: 4, 4: 5}

        def diag_extract(sl, half, vi):
            for jj in range(4):
                j = half * 4 + jj
                h_.tt("dve", junkf, modbc[sl][:, jj * 128:(jj + 1) * 128], identf, ALU.mult, r=["modbc%d" % sl, "identf"], w=["junkf"])
                h_.rsum(vecs[:, vi, j:j + 1], junkf, r=["junkf"], w=["vecs"])

        for nb in range(12):
            sl = nb % 3
            m, half = nb // 2, nb % 2
            for k in range(0, 8, 2):
                h_.dma("pool", wmb[sl][:, k:k + 2, :], w_mod_v[:, k:k + 2, nb * 512:(nb + 1) * 512], w=["wmb%d_%d" % (sl, kk) for kk in range(k, k + 2)])
            h_.dma("sp", bb[sl], b_mod[nb * 512:(nb + 1) * 512].partition_broadcast(128), w=["bb%d" % sl])
            zi = nb % 2
            h_.mms([(Z[zi], csb[:, k, :], wmb[sl][:, k, :], k == 0, k == 7) for k in range(8)],
                   r=["csb"] + ["wmb%d_%d" % (sl, k) for k in range(8)], w=[ZK[zi]])
            if m in (2, 5):
                gb_ = g1_bc if m == 2 else g2_bc
                h_.tt("dve", gb_[:, half * 512:(half + 1) * 512], Z[zi], bb[sl], ALU.add, r=[ZK[zi], "bb%d" % sl], w=["gbc%d_%d" % (m, half)])
            else:
                h_.tt("dve", modbc[sl], Z[zi], bb[sl], ALU.add, r=[ZK[zi], "bb%d" % sl], w=["modbc%d" % sl])
                diag_extract(sl, half, VIDX[m])
            if nb < 4:
                zi2 = 2 + nb % 2
                h_.mms([(Z[zi2], ccb[:, k, :], wmb[sl][:, k, :], k == 0, k == 7) for k in range(8)],
                       r=["ccb"] + ["wmb%d_%d" % (sl, k) for k in range(8)], w=[ZK[zi2]])
                h_.tt("dve", modbc[sl], Z[zi2], bb[sl], ALU.add, r=[ZK[zi2], "bb%d" % sl], w=["modbc%d" % sl])
                diag_extract(sl, half, 2 + m)
        for vi in (1, 3, 5):
            h_.ts("dve", vecs[:, vi, :], vecs[:, vi, :], 1.0, None, ALU.add, r=["vecs"], w=["vecs"])

        for kc in range(2):
            h_.dma("sp", sstg[:, 0:768], w_uq[kc * 128:(kc + 1) * 128, :], w=["sstg"])
            h_.ts("dve", w_uqb[:, kc, :], sstg[:, 0:768], qg[:, kc:kc + 1], None, ALU.mult, r=["sstg", "qg"], w=["w_uqb"])
            h_.dma("sp", sstg2[:, 0:256], w_uq_sw[kc * 128:(kc + 1) * 128, :], w=["sstg2"])
            h_.memset("pool", w_uqsb[:, kc, :], 0.0, w=["w_uqsb"])
            h_.ts("dve", w_uqsb[:, kc, :].rearrange("p (h d) -> p h d", d=96)[:, :, 64:96],
                  sstg2[:, 0:256].rearrange("p (h d) -> p h d", d=32), qg[:, kc:kc + 1], None, ALU.mult,
                  r=["sstg2", "qg", "w_uqsb"], w=["w_uqsb"])
        h_.dma("sp", sstg[:, 0:1024], w_ukv, w=["sstg"])
        ukv_v = sstg[:, 0:1024].rearrange("p (h t d) -> p h t d", t=2, d=64)
        h_.ts("dve", w_kb.rearrange("p (h d) -> p h d", d=64), ukv_v[:, :, 0, :], kvg[:, 0:1], None, ALU.mult, r=["sstg", "kvg"], w=["w_kb"])
        h_.ts("dve", w_vb.rearrange("p (h d) -> p h d", d=64), ukv_v[:, :, 1, :], kvg[:, 0:1], None, ALU.mult, r=["sstg", "kvg"], w=["w_vb"])

        P.barrier()

        R1.reset(); R2.reset()
        cqnT = R2.alloc([2, S], BF16)
        ckvnT = R2.alloc([NKEY], BF16)
        KT = [R2.alloc([NKEY], BF16) for _ in range(2)]
        r2_ab = R2.off
        w_inb = R1.alloc([8, INC], BF16)
        xmT = [R1.alloc([8, 512], BF16) for _ in range(2)]
        U = [R1.alloc([4, 514], F32) for _ in range(2)]
        GB = [R1.alloc([4, 512], F32) for _ in range(2)]
        WB = R2.alloc([8, 96], BF16)
        WC = R2.alloc([8, 96], BF16)
        xt = [R2.alloc([1024], F32) for _ in range(2)]
        xn = [R2.alloc([1024], BF16) for _ in range(2)]
        junk = R2.alloc([1024], BF16)
        sq = [R2.alloc([512], BF16) for _ in range(2)]
        tb = [R2.alloc([512], F32) for _ in range(2)]
        rb = [R2.alloc([512], F32) for _ in range(2)]
        gcs = [R2.alloc([512], F32) for _ in range(2)]
        ycv = [R2.alloc([512], F32) for _ in range(2)]
        convT = [R2.alloc([512], BF16) for _ in range(2)]
        cosb = [R2.alloc([512], F32) for _ in range(2)]
        sinb = [R2.alloc([512], F32) for _ in range(2)]
        rt1 = R2.alloc([512], F32)
        rt2 = R2.alloc([512], F32)
        kst = R2.alloc([8, 32], F32)

        for k in range(8):
            h_.dma("pool", w_inb[:, k, :], w_in[k * 128:(k + 1) * 128, :], w=["w_inb"])
        h_.memset("pool", WB, 0.0, w=["WB"])
        h_.memset("pool", WC, 0.0, w=["WC"])
        h_.copy("pool", WB[:, :, 64:96], w_inb[:, :, 384:416], r=["w_inb", "WB"], w=["WB"])
        h_.dma("sp", kst, w_kr_sw.rearrange("(k p) c -> p k c", p=128), w=["kst"])
        h_.copy("dve", WC[:, :, 64:96], kst, r=["kst", "WC"], w=["WC"])
        for b_ in range(2):
            for c in range(4):
                h_.memset("pool", U[b_][:, c, :], 0.0, w=["U%d_%d" % (b_, c)])
        st_ = dict(g=0, z=0, m=0, q=0, w1=0, stat=0)

        def znext():
            zi = st_["z"] % 4
            st_["z"] += 1
            return zi

        def nt_a(row):
            g = st_["g"]
            st_["g"] += 1
            s2 = g % 2
            gc_ = g % 40
            h_.dma("sp", xt[s2], row, w=["xt%d" % s2])
            h_.act(junk, xt[s2], AF.Square, r=["xt%d" % s2], w=["junk", "ssA%d" % gc_], accum=ssA[:, gc_:gc_ + 1])
            h_.act(ssB[:, gc_:gc_ + 1], ssA[:, gc_:gc_ + 1], AF.Ln, r=["ssA%d" % gc_, "epsb"], w=["ssB%d" % gc_], scale=1.0 / D, bias=epsb[:, 0:1])
            h_.act(rsA[:, gc_:gc_ + 1], ssB[:, gc_:gc_ + 1], AF.Exp, r=["ssB%d" % gc_], w=["rsA%d" % gc_], scale=-0.5)
            h_.act(xn[s2], xt[s2], AF.Copy, r=["xt%d" % s2, "rsA%d" % gc_], w=["xn%d" % s2], scale=rsA[:, gc_:gc_ + 1])
            return s2

        def nt_b(s2, i, xm, sci, shi, xmkey):
            tpb = Mps[s2].bitcast(BF16)
            h_.trs([(tpb[:, j * 128:(j + 1) * 128], xn[s2][:, j * 128:(j + 1) * 128]) for j in range(8)], idb, r=["xn%d" % s2, "idb"], w=["M%d" % s2])
            for j in range(8):
                if j % 2 == 0:
                    h_.ts("dve", xm[:, j, i * 128:(i + 1) * 128], tpb[:, j * 128:(j + 1) * 128], vecs[:, sci, j:j + 1], vecs[:, shi, j:j + 1],
                          ALU.mult, ALU.add, r=["M%d" % s2, "vecs"], w=[xmkey])
                else:
                    h_.act(xm[:, j, i * 128:(i + 1) * 128], tpb[:, j * 128:(j + 1) * 128], AF.Identity, r=["M%d" % s2, "vecs"], w=[xmkey],
                           scale=vecs[:, sci, j:j + 1], bias=vecs[:, shi, j:j + 1])

        def zmm(zi, wcol_lo, width, xm, ncol, xmkey, wt=None):
            lst = []
            for k in range(8):
                lhs = w_inb[:, k, wcol_lo:wcol_lo + width] if wt is None else wt[:, k, 0:width]
                lst.append((Z[zi][0:width, 0:ncol], lhs, xm[:, k, 0:ncol], k == 0, k == 7))
            h_.mms(lst, r=[xmkey, "w_inb", "WB", "WC"], w=[ZK[zi]])

        def rms_bc(zis, nfeat, ncol, outs, okeys, osum, okey, s2):
            n = len(zis)
            for i, zi in enumerate(zis):
                h_.act(sq[i][:, 0:ncol], Z[zi][:, 0:ncol], AF.Square, r=[ZK[zi]], w=["sq%d" % i])
            h_.mms([(osum[:, 0:ncol], ones_bf, sq[i][:, 0:ncol], i == 0, i == n - 1) for i in range(n)],
                   r=["ones_bf"] + ["sq%d" % i for i in range(n)], w=[okey])
            h_.act(tb[s2][:, 0:ncol], osum[:, 0:ncol], AF.Ln, r=[okey, "epsb"], w=["tb%d" % s2], scale=1.0 / nfeat, bias=epsb[:, 0:1])
            h_.act(rb[s2][:, 0:ncol], tb[s2][:, 0:ncol], AF.Exp, r=["tb%d" % s2], w=["rb%d" % s2], scale=-0.5)
            for i, zi in enumerate(zis):
                h_.tt("dve", outs[i], Z[zi][:, 0:ncol], rb[s2][:, 0:ncol], ALU.mult, r=[ZK[zi], "rb%d" % s2], w=[okeys[i]])

        def conv_finalize(t):
            s2 = t % 2
            for c in range(4):
                y2 = c % 2
                uk = "U%d_%d" % (s2, c)
                yk = "ycv%d" % y2
                h_.ts("dve", ycv[y2], U[s2][:, c, 1:513], convw[:, c * 3 + 1:c * 3 + 2], None, ALU.mult, r=[uk, "convw"], w=[yk])
                h_.stt(ycv[y2], U[s2][:, c, 0:512], convw[:, c * 3:c * 3 + 1], ycv[y2], ALU.mult, ALU.add, r=[uk, "convw", yk], w=[yk])
                h_.stt(ycv[y2], U[s2][:, c, 2:514], convw[:, c * 3 + 2:c * 3 + 3], ycv[y2], ALU.mult, ALU.add, r=[uk, "convw", yk], w=[yk])
                h_.tt("pool", convT[y2], GB[s2][:, c, :], ycv[y2], ALU.mult, r=["GB%d_%d" % (s2, c), yk], w=["convT%d" % y2])
                h_.dma("pool", acT[512 + c * 128:512 + (c + 1) * 128, t * 512:(t + 1) * 512], convT[y2], r=["convT%d" % y2], w=["acT_c%d" % t])

        def blk_rows(t):
            if t == 8:
                return [ctx[i * 128:(i + 1) * 128, :] for i in range(2)]
            return [x[(t * 4 + i) * 128:(t * 4 + i + 1) * 128, :] for i in range(4)]

        def nt_units(t):
            xb = t % 2
            sci, shi = (3, 2) if t == 8 else (1, 0)
            out = []
            slot = {}
            for i, row in enumerate(blk_rows(t)):
                def fa(i=i, row=row):
                    slot[i] = nt_a(row)

                def fb(i=i):
                    nt_b(slot[i], i, xmT[xb], sci, shi, "xmT%d" % xb)
                out.append((fa, fb))
            return out

        def z_units(t):
            isctx = (t == 8)
            ncol = 256 if isctx else 512
            xb = t % 2
            xmkey = "xmT%d" % xb
            k0 = t * 512
            units = []

            def u_ckv():
                if not isctx:
                    h_.dma("sp", cosb[xb][64:96, :], cos_t[:, k0:k0 + 512], w=["cosb%d" % xb])
                    h_.dma("sp", sinb[xb][64:96, :], sin_t[:, k0:k0 + 512], w=["sinb%d" % xb])
                zi = znext()
                zmm(zi, 256, 128, xmT[xb], ncol, xmkey)
                rms_bc([zi], 128, ncol, [ckvnT[:, k0:k0 + ncol]], ["ckvnT%d" % t], Ops[0], "O0", 0)
            units.append(u_ckv)

            def u_kr():
                ziB = znext()
                zmm(ziB, 0, 96, xmT[xb], ncol, xmkey, wt=WB)
                if isctx:
                    h_.copy("dve", KT[0][64:96, k0:k0 + ncol], Z[ziB][64:96, 0:ncol], r=[ZK[ziB]], w=["KTr0_%d" % t])
                else:
                    ziC = znext()
                    zmm(ziC, 0, 96, xmT[xb], ncol, xmkey, wt=WC)
                    h_.tt("dve", rt1[64:96, :], Z[ziB][64:96, :], cosb[xb][64:96, :], ALU.mult, r=[ZK[ziB], "cosb%d" % xb], w=["rt1"])
                    h_.tt("dve", rt2[64:96, :], Z[ziC][64:96, :], sinb[xb][64:96, :], ALU.mult, r=[ZK[ziC], "sinb%d" % xb], w=["rt2"])
                    h_.tt("dve", KT[0][64:96, k0:k0 + 512], rt1[64:96, :], rt2[64:96, :], ALU.add, r=["rt1", "rt2"], w=["KTr0_%d" % t])
                h_.copy("pool", KT[1][64:96, k0:k0 + ncol], KT[0][64:96, k0:k0 + ncol], r=["KTr0_%d" % t], w=["KTr1_%d" % t])
            units.append(u_kr)
            if isctx:
                return units

            def u_q():
                z0 = znext()
                zmm(z0, 0, 128, xmT[xb], ncol, xmkey)
                z1 = znext()
                zmm(z1, 128, 128, xmT[xb], ncol, xmkey)
                rms_bc([z0, z1], 256, ncol, [cqnT[:, 0, k0:k0 + 512], cqnT[:, 1, k0:k0 + 512]], ["cqnT%d" % t, "cqnT%d" % t], Ops[1], "O1", 1)
            units.append(u_q)
            pb = 1 - xb

            def u_conv(c):
                g2 = c % 2
                uk = "U%d_%d" % (xb, c)
                pk = "U%d_%d" % (pb, c)
                zb = znext()
                zmm(zb, 416 + c * 128, 128, xmT[xb], ncol, xmkey)
                h_.copy("act", GB[xb][:, c, :], Z[zb], r=[ZK[zb]], w=["GB%d_%d" % (xb, c)])
                zc = znext()
                zmm(zc, 928 + c * 128, 128, xmT[xb], ncol, xmkey)
                h_.copy("act", gcs[g2], Z[zc], r=[ZK[zc]], w=["gcs%d" % g2])
                zx = znext()
                zmm(zx, 1440 + c * 128, 128, xmT[xb], ncol, xmkey)
                h_.tt("dve", U[xb][:, c, 1:513], Z[zx], gcs[g2], ALU.mult, r=[ZK[zx], "gcs%d" % g2], w=[uk])
                if t > 0:
                    h_.copy("pool", U[xb][:, c, 0:1], U[pb][:, c, 512:513], r=[pk, uk], w=[uk])
                    h_.copy("pool", U[pb][:, c, 513:514], U[xb][:, c, 1:2], r=[uk, pk], w=[pk])
                else:
                    h_.memset("pool", U[xb][:, c, 0:1], 0.0, r=[uk], w=[uk])
            for c in range(4):
                units.append(lambda c=c: u_conv(c))

            def u_fin():
                if t > 0:
                    conv_finalize(t - 1)
                if t == 7:
                    for c in range(4):
                        h_.memset("pool", U[xb][:, c, 513:514], 0.0, r=["U%d_%d" % (xb, c)], w=["U%d_%d" % (xb, c)])
                    conv_finalize(7)
            units.append(u_fin)
            return units

        for fa, fb in nt_units(0):
            fa()
            fb()
        for t in range(9):
            nxt = nt_units(t + 1) if t + 1 < 9 else []
            na = 0
            nb_ = 0
            for idx, u in enumerate(z_units(t)):
                u()
                if idx >= 2 and nb_ < na:
                    nxt[nb_][1]()
                    nb_ += 1
                if na < len(nxt):
                    nxt[na][0]()
                    na += 1
            while nb_ < len(nxt):
                if na <= nb_:
                    nxt[na][0]()
                    na += 1
                nxt[nb_][1]()
                nb_ += 1
                if na < len(nxt):
                    nxt[na][0]()
                    na += 1

        P.barrier()

        R1.reset(); R2.reset(r2_ab)
        w_outb = R1.alloc([8, 1024], BF16)
        w2b = R1.alloc([32, 1024], BF16)
        V = [R2.alloc([NKT, 65], BF16) for _ in range(2)]
        qT = [R2.alloc([S], BF16) for _ in range(2)]
        PT = [R2.alloc([1024], BF16) for _ in range(3)]
        qcos = [R2.alloc([512], F32) for _ in range(2)]
        qsin = [R2.alloc([512], F32) for _ in range(2)]
        qt1 = R2.alloc([512], F32)
        qt2 = R2.alloc([512], F32)
        den = R2.alloc([512], F32)
        rec = R2.alloc([512], F32)
        aoT = [R2.alloc([512], BF16) for _ in range(2)]
        bstg = [R2.alloc([1024], F32) for _ in range(2)]

        for b_ in range(2):
            h_.memset("pool", V[b_][:, :, 64:65], 1.0, w=["Vones%d" % b_])

        wq = [("o", j) for j in range(8)] + [("2", j) for j in range(32)]

        w1tmp = w2b[:, 28:32, :]
        w1s_w = w1s.rearrange("f p (k c) -> p f k c", k=8)

        def precast_in(k):
            h_.dma("pool", w1tmp, w1[k * 128:(k + 1) * 128, :].rearrange("p (a b) -> p a b", b=1024), w=["w1tmp"])

        def precast_out(k):
            h_.dma("sp", w1s_w[:, :, k, :], w1tmp.rearrange("p a (g c) -> p (a g) c", c=128), r=["w1tmp"], w=["w1s"])

        def load_some_weights(n):
            for _ in range(n):
                if st_["q"] >= len(wq):
                    return
                kind, j = wq[st_["q"]]
                s2 = st_["q"] % 2
                st_["q"] += 1
                if kind == "o":
                    h_.dma("sp", bstg[s2], w_out[j * 128:(j + 1) * 128, :], w=["bstg%d" % s2])
                    h_.tt("pool", w_outb[:, j, :], bstg[s2], g1_bc, ALU.mult, r=["bstg%d" % s2], w=["w_outb"])
                else:
                    h_.dma("sp", bstg[s2], w2[j * 128:(j + 1) * 128, :], w=["bstg%d" % s2])
                    h_.tt("pool", w2b[:, j, :], bstg[s2], g2_bc, ALU.mult, r=["bstg%d" % s2], w=["w2b"] + (["w1tmp"] if j >= 28 else []))

        def mnext():
            m = st_["m"] % 2
            st_["m"] += 1
            return m

        NP = NKT // 2
        qdc = [0]

        def setup_units(h):
            hb = h % 2
            units = []

            def kunit(kb):
                n = 512 if kb < 8 else 256
                m = mnext()
                h_.mms([(Mps[m][0:64, 0:n], w_kb[:, h * 64:(h + 1) * 64], ckvnT[:, kb * 512:kb * 512 + n], True, True)], r=["w_kb", "ckvnT%d" % kb], w=["M%d" % m])
                h_.copy("dve", KT[hb][0:64, kb * 512:kb * 512 + n], Mps[m][0:64, 0:n], r=["M%d" % m], w=["KTn%d_%d" % (hb, kb)])

            def vunit(g0):
                ng = min(8, NKT - g0)
                m = mnext()
                h_.mms([(Mps[m][:, jj * 64:(jj + 1) * 64], ckvnT[:, (g0 + jj) * 128:(g0 + jj + 1) * 128], w_vb[:, h * 64:(h + 1) * 64], True, True) for jj in range(ng)],
                       r=["w_vb"] + ["ckvnT%d" % kb for kb in range(9)], w=["M%d" % m])
                h_.copy("dve", V[hb][:, g0:g0 + ng, 0:64], Mps[m][:, 0:ng * 64].rearrange("p (a b) -> p a b", b=64), r=["M%d" % m], w=["V%d" % hb])

            def qunit(qb):
                q0 = qb * 512
                cs2 = qdc[0] % 2
                qdc[0] += 1
                h_.dma("sp", qcos[cs2][64:96, :], cos_t[:, q0:q0 + 512], w=["qcos%d" % cs2])
                h_.dma("sp", qsin[cs2][64:96, :], sin_t[:, q0:q0 + 512], w=["qsin%d" % cs2])
                ma = mnext()
                mb = mnext()
                h_.mms([(Mps[ma][0:96, :], w_uqb[:, kc, h * 96:(h + 1) * 96], cqnT[:, kc, q0:q0 + 512], kc == 0, kc == 1) for kc in range(2)], r=["w_uqb", "cqnT%d" % qb], w=["M%d" % ma])
                h_.mms([(Mps[mb][0:96, :], w_uqsb[:, kc, h * 96:(h + 1) * 96], cqnT[:, kc, q0:q0 + 512], kc == 0, kc == 1) for kc in range(2)], r=["w_uqsb", "cqnT%d" % qb], w=["M%d" % mb])
                qk_ = "qT%d_%d" % (hb, qb)
                h_.copy("dve", qT[hb][0:64, q0:q0 + 512], Mps[ma][0:64, :], r=["M%d" % ma], w=[qk_])
                h_.tt("dve", qt1[64:96, :], Mps[ma][64:96, :], qcos[cs2][64:96, :], ALU.mult, r=["M%d" % ma, "qcos%d" % cs2], w=["qt1"])
                h_.tt("dve", qt2[64:96, :], Mps[mb][64:96, :], qsin[cs2][64:96, :], ALU.mult, r=["M%d" % mb, "qsin%d" % cs2], w=["qt2"])
                h_.tt("dve", qT[hb][64:96, q0:q0 + 512], qt1[64:96, :], qt2[64:96, :], ALU.add, r=["qt1", "qt2"], w=[qk_])

            for kb in range(9):
                units.append(lambda kb=kb: kunit(kb))
            for g0 in range(0, NKT, 8):
                units.append(lambda g0=g0: vunit(g0))
            for qb in range(8):
                units.append(lambda qb=qb: qunit(qb))
            return units

        osb = R2.alloc([512], F32)
        steps = [(h, qb, jp) for h in range(NH) for qb in range(8) for jp in range(NP)]
        NS_ = len(steps)

        def qk(s):
            h, qb, jp = steps[s]
            hb = h % 2
            q0 = qb * 512
            sp_ = s % 2
            lst = []
            keys = ["qT%d_%d" % (hb, qb)]
            for u in range(2):
                j = 2 * jp + u
                kb = (j * 128) // 512
                keys += ["KTn%d_%d" % (hb, kb), "KTr%d_%d" % (hb, kb)]
                lst.append((Sps[sp_][:, u * 512:(u + 1) * 512], KT[hb][0:96, j * 128:(j + 1) * 128], qT[hb][0:96, q0:q0 + 512], True, True))
            h_.mms(lst, r=keys, w=["S%d" % sp_])

        def ex(s):
            sp_ = s % 2
            pp = s % 3
            h_.act(PT[pp], Sps[sp_], AF.Exp, r=["S%d" % sp_], w=["PT%d" % pp], scale=ATTN_SCALE)

        def pv(s):
            h, qb, jp = steps[s]
            hb = h % 2
            ob = qb % 2
            pp = s % 3
            lst = []
            for u in range(2):
                j = 2 * jp + u
                lst.append((Ops[ob][0:65, :], V[hb][:, j, 0:65], PT[pp][:, u * 512:(u + 1) * 512], j == 0, j == NKT - 1))
            h_.mms(lst, r=["V%d" % hb, "Vones%d" % hb, "PT%d" % pp], w=["O%d" % ob])

        def norm1(h, qb):
            ob = qb % 2
            h_.copy("dve", den[64:65, :], Ops[ob][64:65, :], r=["O%d" % ob], w=["den"])
            h_.copy("dve", osb[0:64, :], Ops[ob][0:64, :], r=["O%d" % ob], w=["osb"])

        def norm2(h, qb):
            m = mnext()
            h_.mms([(Mps[m][0:64, :], onesf[64:65, 0:64], den[64:65, :], True, True)], r=["den", "onesf"], w=["M%d" % m])
            h_.recip(rec[0:64, :], Mps[m][0:64, :], r=["M%d" % m], w=["rec"])
            a2 = (h * 8 + qb) % 2
            h_.tt("dve", aoT[a2][0:64, :], osb[0:64, :], rec[0:64, :], ALU.mult, r=["osb", "rec"], w=["aoT%d" % a2])
            h_.dma("pool", acT[h * 64:(h + 1) * 64, qb * 512:(qb + 1) * 512], aoT[a2][0:64, :], r=["aoT%d" % a2], w=["acT_a%d_%d" % (h, qb)])

        for u in setup_units(0):
            u()
        qk(0)
        ex(0)
        qk(1)
        ex(1)
        pending = None
        nxt = []
        cnt = 0
        for s in range(NS_):
            h, qb, jp = steps[s]
            if jp == 0 and h < 4:
                if qb == 0:
                    precast_in(2 * h)
                elif qb == 2:
                    precast_out(2 * h)
                elif qb == 3:
                    precast_in(2 * h + 1)
                elif qb == 5:
                    precast_out(2 * h + 1)
            if jp == 0 and qb == 0:
                load_some_weights(5)
                nxt = setup_units(h + 1) if h + 1 < NH else []
                cnt = 0
            if s + 2 < NS_:
                if steps[s + 2][0] != h:
                    while nxt:
                        nxt.pop(0)()
                qk(s + 2)
            pv(s)
            if s + 2 < NS_:
                ex(s + 2)
            if pending is not None and pending[0] == s:
                norm2(pending[1], pending[2])
                pending = None
            if jp == NP - 1:
                norm1(h, qb)
                pending = (s + 2, h, qb)
            cnt += 1
            if nxt and cnt % 5 == 3:
                nxt.pop(0)()
        if pending is not None:
            norm2(pending[1], pending[2])
        load_some_weights(100)

        P.barrier()

        R2.reset()
        xs = [[R2.alloc([1024], F32) for _ in range(4)] for _ in range(2)]
        acb = R2.alloc([8, 512], BF16)
        xn2 = [R2.alloc([1024], BF16) for _ in range(4)]
        xm2T = R2.alloc([8, 512], BF16)
        hT = R2.alloc([32, 512], BF16)
        hr = [R2.alloc([512], F32) for _ in range(2)]
        junk2 = hr[0].bitcast(BF16)
        w1c = [R2.alloc([8, 128], BF16) for _ in range(5)]
        w1_v = w1.rearrange("(k p) n -> p k n", p=128)

        def stats(src, srckey):
            g = st_["stat"] % 40
            st_["stat"] += 1
            h_.act(junk2, src, AF.Square, r=[srckey], w=["hr0", "ssA%d" % g], accum=ssA[:, g:g + 1])
            h_.act(ssB[:, g:g + 1], ssA[:, g:g + 1], AF.Ln, r=["ssA%d" % g, "epsb"], w=["ssB%d" % g], scale=1.0 / D, bias=epsb[:, 0:1])
            h_.act(rsA[:, g:g + 1], ssB[:, g:g + 1], AF.Exp, r=["ssB%d" % g], w=["rsA%d" % g], scale=-0.5)
            return g

        def load_block(t):
            for i in range(4):
                h_.dma("sp", xs[t % 2][i], x[t * 512 + i * 128:t * 512 + (i + 1) * 128, :], w=["x%d_%d" % (t % 2, i)])

        def load_acb(t):
            for j in range(8):
                h_.dma("sp", acb[:, j, :], acT[j * 128:(j + 1) * 128, t * 512:(t + 1) * 512], w=["acb"])

        def outproj_stats(t):
            X = xs[t % 2]
            for i in range(4):
                xk = "x%d_%d" % (t % 2, i)
                for hf in range(2):
                    zi = znext()
                    h_.mms([(Z[zi], acb[:, j, i * 128:(i + 1) * 128], w_outb[:, j, hf * 512:(hf + 1) * 512], j == 0, j == 7) for j in range(8)], r=["acb", "w_outb"], w=[ZK[zi]])
                    h_.tt("dve", X[i][:, hf * 512:(hf + 1) * 512], Z[zi], X[i][:, hf * 512:(hf + 1) * 512], ALU.add, r=[ZK[zi], xk], w=[xk])
                g = stats(X[i], xk)
                h_.act(xn2[i], X[i], AF.Copy, r=[xk, "rsA%d" % g], w=["xn2_%d" % i], scale=rsA[:, g:g + 1])

        def trans(i):
            s2 = i % 2
            tpb = Mps[s2].bitcast(BF16)
            h_.trs([(tpb[:, j * 128:(j + 1) * 128], xn2[i][:, j * 128:(j + 1) * 128]) for j in range(8)], idb, r=["xn2_%d" % i, "idb"], w=["M%d" % s2])
            for j in range(8):
                if j % 2 == 0:
                    h_.ts("dve", xm2T[:, j, i * 128:(i + 1) * 128], tpb[:, j * 128:(j + 1) * 128], vecs[:, 5, j:j + 1], vecs[:, 4, j:j + 1],
                          ALU.mult, ALU.add, r=["M%d" % s2, "vecs"], w=["xm2T"])
                else:
                    h_.act(xm2T[:, j, i * 128:(i + 1) * 128], tpb[:, j * 128:(j + 1) * 128], AF.Identity, r=["M%d" % s2, "vecs"], w=["xm2T"],
                           scale=vecs[:, 5, j:j + 1], bias=vecs[:, 4, j:j + 1])

        def mlp1(t, hook=None):
            for f_ in range(32):
                if f_ == 8 and hook is not None:
                    hook()
                n = st_["w1"]
                st_["w1"] += 1
                c4 = n % 5
                h_.dma("sp", w1c[c4], w1s[f_].rearrange("p (k c) -> p k c", k=8), w=["w1c%d" % c4])
                zi = znext()
                h_.mms([(Z[zi], w1c[c4][:, k, :], xm2T[:, k, :], k == 0, k == 7) for k in range(8)], r=["w1c%d" % c4, "xm2T"], w=[ZK[zi]])
                h2 = f_ % 2
                h_.act(hr[h2], Z[zi], AF.Relu, r=[ZK[zi]], w=["hr%d" % h2])
                h_.tt("dve", hT[:, f_, :], hr[h2], hr[h2], ALU.mult, r=["hr%d" % h2], w=["hT%d" % f_])

        def mlp2_final(t, with_trans):
            X = xs[t % 2]
            for i in range(4):
                xk = "x%d_%d" % (t % 2, i)
                for hf in range(2):
                    zi = znext()
                    h_.mms([(Z[zi], hT[:, k, i * 128:(i + 1) * 128], w2b[:, k, hf * 512:(hf + 1) * 512], k == 0, k == 31) for k in range(32)],
                           r=["w2b"] + ["hT%d" % k for k in range(32)], w=[ZK[zi]])
                    h_.tt("dve", X[i][:, hf * 512:(hf + 1) * 512], Z[zi], X[i][:, hf * 512:(hf + 1) * 512], ALU.add, r=[ZK[zi], xk], w=[xk])
                if with_trans:
                    trans(i)
                g = stats(X[i], xk)
                h_.stt(X[i], X[i], rsA[:, g:g + 1], gf_bc, ALU.mult, ALU.mult, r=[xk, "rsA%d" % g, "gf_bc"], w=[xk])
                h_.dma("pool", y[t * 512 + i * 128:t * 512 + (i + 1) * 128, :], X[i], r=[xk], w=["y%d_%d" % (t, i)])

        load_block(0)
        load_acb(0)
        load_block(1)
        outproj_stats(0)
        load_acb(1)
        for i in range(4):
            trans(i)
        for t in range(8):
            mlp1(t, hook=((lambda t=t: load_block(t + 1)) if (t >= 1 and t + 1 < 8) else None))
            if t + 1 < 8:
                outproj_stats(t + 1)
                if t + 2 < 8:
                    load_acb(t + 2)
            mlp2_final(t, t + 1 < 8)

        P.emit(nc, block, st)
    return nc


def _host_inputs(x, c, ctx, c_ctx, w_mod, b_mod, w_in, q_norm_g, w_uq, kv_norm_g, w_ukv,
                 conv_w, w_out, w_mlp1, w_mlp2, final_norm_g):
    f = lambda a: np.ascontiguousarray(np.asarray(a), dtype=np.float32)
    x, c, ctx, c_ctx = f(x), f(c), f(ctx), f(c_ctx)
    w_mod, b_mod, w_in = f(w_mod)[0], f(b_mod)[0], f(w_in)[0]
    q_norm_g, w_uq, kv_norm_g, w_ukv = f(q_norm_g)[0], f(w_uq)[0], f(kv_norm_g)[0], f(w_ukv)[0]
    conv_w, w_out, w_mlp1, w_mlp2, gfin = f(conv_w)[0], f(w_out)[0], f(w_mlp1)[0], f(w_mlp2)[0], f(final_norm_g)
    perm = np.array(list(range(8, 16)) + list(range(0, 8)) + list(range(24, 32)) + list(range(16, 24)))
    w_kr_sw = f(w_in[:, 384:416][:, perm])
    w_uq_sw = f(w_uq.reshape(256, 8, 96)[:, :, 64:96][:, :, perm].reshape(256, 256))
    pos = np.arange(S)
    row = (pos // 64).astype(np.float32)
    col = (pos % 64).astype(np.float32)
    freqs = (np.float32(10000.0) ** (-np.arange(0, 16, 2, dtype=np.float32) / np.float32(16))).astype(np.float32)
    ar = (row[None, :] * freqs[:, None]).astype(np.float32)
    ac = (col[None, :] * freqs[:, None]).astype(np.float32)
    cos_t = np.concatenate([np.cos(ar), np.cos(ar), np.cos(ac), np.cos(ac)], 0).astype(np.float32)
    sin_t = np.concatenate([-np.sin(ar), np.sin(ar), -np.sin(ac), np.sin(ac)], 0).astype(np.float32)
    shared = dict(
        cc_fm=f(c_ctx.reshape(8, 128).T), w_mod=w_mod, b_mod=b_mod, w_in=w_in, w_kr_sw=w_kr_sw,
        qg_fm=f(q_norm_g.reshape(2, 128).T), w_uq=w_uq, w_uq_sw=w_uq_sw, kvg_fm=f(kv_norm_g.reshape(1, 128).T),
        w_ukv=w_ukv, convw_fm=f(conv_w.reshape(3, 4, 128).transpose(2, 1, 0).reshape(128, 12)),
        w_out=w_out, w_mlp1=w_mlp1, w_mlp2=w_mlp2, gf=gfin, ident=np.eye(128, dtype=np.float32),
        cos_t=cos_t, sin_t=sin_t)
    maps = []
    for b in range(8):
        m = dict(shared)
        m["x"] = x[b]
        m["ctx"] = ctx[b]
        m["c_fm"] = f(c[b].reshape(8, 128).T)
        maps.append(m)
    return maps


_NC = None


def kernel(**inputs):
    global _NC
    maps = _host_inputs(**inputs)
    if _NC is None:
        _NC = build()
    res = run_bass_kernel_spmd(_NC, maps, core_ids=list(range(8)))
    return np.stack([np.asarray(r["y"]) for r in res.results], 0).astype(np.float32)
```
